# Optimizing a Trainium2 kernel written in Bass

```python
import jax, jax.numpy as jnp
from jax import lax
import numpy as np

D_MODEL = 1024
BATCH = 8
SEQ = 2048
DEPTH = 1

MIX_WIDTH = D_MODEL
CONV_WIDTH = MIX_WIDTH // 2
CONV_GROUPS = 8
CONV_K = 3
RET_WIDTH = MIX_WIDTH - CONV_WIDTH
RET_HEADS = 4
RET_HEAD_DIM = RET_WIDTH // RET_HEADS
RET_CHUNK = 128
ROPE_BASE = 10000.0
N_IN_COLS = 3 * CONV_WIDTH + 4 * RET_WIDTH
D_FF = -(-8 * D_MODEL // (3 * 256)) * 256
DN_ALPHA = float((2 * DEPTH) ** 0.25)
DN_BETA = float((8 * DEPTH) ** -0.25)
LN_EPS = 1e-5
N_MOD = 6

kernel_name = "hybrid_shortconv_retention_deepnorm_adaln"


def _layernorm(x, g, b):
    xf = x.astype(jnp.float32)
    mu = jnp.mean(xf, axis=-1, keepdims=True)
    var = jnp.mean(jnp.square(xf - mu), axis=-1, keepdims=True)
    y = (xf - mu) * lax.rsqrt(var + LN_EPS)
    return (y * g.astype(jnp.float32) + b.astype(jnp.float32)).astype(x.dtype)


def _head_groupnorm(y):
    yf = y.astype(jnp.float32)
    mu = jnp.mean(yf, axis=-1, keepdims=True)
    var = jnp.mean(jnp.square(yf - mu), axis=-1, keepdims=True)
    return ((yf - mu) * lax.rsqrt(var + LN_EPS)).astype(y.dtype)


def _rotary(t, seq_len):
    d = t.shape[-1]
    inv_freq = ROPE_BASE ** (-jnp.arange(0, d, 2, dtype=jnp.float32) / d)
    ang = jnp.arange(seq_len, dtype=jnp.float32)[:, None] * inv_freq[None, :]
    cos = jnp.cos(ang)[None, :, None, :].astype(t.dtype)
    sin = jnp.sin(ang)[None, :, None, :].astype(t.dtype)
    t1, t2 = t[..., : d // 2], t[..., d // 2:]
    return jnp.concatenate([t1 * cos - t2 * sin, t1 * sin + t2 * cos], axis=-1)


def _short_conv_mixer(cg, xin, bg, conv_w):
    u = cg * xin
    up = jnp.pad(u, ((0, 0), (CONV_K - 1, 0), (0, 0)))
    s = u.shape[1]
    y = conv_w[0] * up[:, 0:s] + conv_w[1] * up[:, 1:s + 1] + conv_w[2] * up[:, 2:s + 2]
    return bg * y


def _retention(q, k, v):
    bsz, s, h, d = q.shape
    L = RET_CHUNK
    n = s // L
    dt = q.dtype
    log_g = jnp.log1p(-(2.0 ** (-5.0 - jnp.arange(h, dtype=jnp.float32))))
    idx = jnp.arange(L, dtype=jnp.float32)
    diff = idx[:, None] - idx[None, :]
    decay_mask = jnp.where(diff >= 0, jnp.exp(log_g[:, None, None] * jnp.maximum(diff, 0.0)), 0.0).astype(dt)
    zeta = jnp.exp(log_g[:, None] * (L - 1 - idx)[None, :]).astype(dt)
    xi = jnp.exp(log_g[:, None] * (idx + 1)[None, :]).astype(dt)
    chunk_decay = jnp.exp(log_g * L).astype(dt)

    qc = q.reshape(bsz, n, L, h, d)
    kc = k.reshape(bsz, n, L, h, d)
    vc = v.reshape(bsz, n, L, h, d)

    scores = jnp.einsum('bnihd,bnjhd->bhnij', qc, kc) * decay_mask[None, :, None]
    intra = jnp.einsum('bhnij,bnjhe->bnihe', scores, vc)

    chunk_kv = jnp.einsum('bnjhd,hj,bnjhe->nbhde', kc, zeta, vc)

    def step(state, kv):
        return kv + chunk_decay[None, :, None, None] * state, state

    init = jnp.zeros((bsz, h, d, d), dtype=dt)
    _, prev_states = lax.scan(step, init, chunk_kv)
    inter = jnp.einsum('bnihd,hi,nbhde->bnihe', qc, xi, prev_states)
    return (intra + inter).reshape(bsz, s, h, d)


def _hybrid_mixer(h, w_in, conv_w, w_out):
    bsz, s, _ = h.shape
    z = h @ w_in
    wc, wr = CONV_WIDTH, RET_WIDTH
    cg = z[..., 0:wc]
    xin = z[..., wc:2 * wc]
    bg = z[..., 2 * wc:3 * wc]
    o = 3 * wc
    q = z[..., o:o + wr].reshape(bsz, s, RET_HEADS, RET_HEAD_DIM)
    k = z[..., o + wr:o + 2 * wr].reshape(bsz, s, RET_HEADS, RET_HEAD_DIM)
    v = z[..., o + 2 * wr:o + 3 * wr].reshape(bsz, s, RET_HEADS, RET_HEAD_DIM)
    g = z[..., o + 3 * wr:o + 4 * wr]

    y_conv = _short_conv_mixer(cg, xin, bg, conv_w)

    q = _rotary(q, s)
    k = _rotary(k, s) * (RET_HEAD_DIM ** -0.5)
    r = _head_groupnorm(_retention(q, k, v)).reshape(bsz, s, wr)
    y_ret = jax.nn.silu(g) * r

    return jnp.concatenate([y_conv, y_ret], axis=-1) @ w_out


def _swiglu(h, w_gate, w_up, w_down):
    return (jax.nn.silu(h @ w_gate) * (h @ w_up)) @ w_down


def setup_inputs(seed: int = 0) -> dict:
    key = jax.random.key(seed)
    ks = jax.random.split(key, 16)
    d = D_MODEL
    f32 = jnp.float32
    x = jax.random.normal(ks[0], (BATCH, SEQ, d), f32)
    c = jax.random.normal(ks[1], (BATCH, d), f32)
    ada_w = jax.random.normal(ks[2], (DEPTH, d, N_MOD * d), f32) * d ** -0.5
    ada_b = jax.random.normal(ks[3], (DEPTH, N_MOD * d), f32) * 0.01
    w_in = jax.random.normal(ks[4], (DEPTH, d, N_IN_COLS), f32) * d ** -0.5
    v_lo = 3 * CONV_WIDTH + 2 * RET_WIDTH
    w_in = w_in.at[:, :, v_lo:v_lo + RET_WIDTH].multiply(DN_BETA)
    conv_w = jax.random.normal(ks[5], (DEPTH, CONV_K, CONV_WIDTH), f32) * CONV_K ** -0.5
    w_out = jax.random.normal(ks[6], (DEPTH, MIX_WIDTH, d), f32) * MIX_WIDTH ** -0.5 * DN_BETA
    ln1_g = 1.0 + 0.01 * jax.random.normal(ks[7], (DEPTH, d), f32)
    ln1_b = 0.01 * jax.random.normal(ks[8], (DEPTH, d), f32)
    w_gate = jax.random.normal(ks[9], (DEPTH, d, D_FF), f32) * d ** -0.5
    w_up = jax.random.normal(ks[10], (DEPTH, d, D_FF), f32) * d ** -0.5
    w_down = jax.random.normal(ks[11], (DEPTH, D_FF, d), f32) * D_FF ** -0.5 * DN_BETA
    ln2_g = 1.0 + 0.01 * jax.random.normal(ks[12], (DEPTH, d), f32)
    ln2_b = 0.01 * jax.random.normal(ks[13], (DEPTH, d), f32)
    return {"x": x, "c": c, "ada_w": ada_w, "ada_b": ada_b, "w_in": w_in,
            "conv_w": conv_w, "w_out": w_out, "ln1_g": ln1_g, "ln1_b": ln1_b,
            "w_gate": w_gate, "w_up": w_up, "w_down": w_down,
            "ln2_g": ln2_g, "ln2_b": ln2_b}


def reference(x, c, ada_w, ada_b, w_in, conv_w, w_out, ln1_g, ln1_b,
              w_gate, w_up, w_down, ln2_g, ln2_b):
    sc = jax.nn.silu(c)
    for layer in range(DEPTH):
        mod = sc @ ada_w[layer] + ada_b[layer]
        shift_m, scale_m, gate_m, shift_f, scale_f, gate_f = [
            m[:, None, :] for m in jnp.split(mod, N_MOD, axis=-1)]
        h = x * (1.0 + scale_m) + shift_m
        mix = _hybrid_mixer(h, w_in[layer], conv_w[layer], w_out[layer])
        x = _layernorm(DN_ALPHA * x + gate_m * mix, ln1_g[layer], ln1_b[layer])
        h = x * (1.0 + scale_f) + shift_f
        ff = _swiglu(h, w_gate[layer], w_up[layer], w_down[layer])
        x = _layernorm(DN_ALPHA * x + gate_f * ff, ln2_g[layer], ln2_b[layer])
    return x
```

```python
import contextlib
import numpy as np
import ml_dtypes
import concourse.bass as bass
import concourse.mybir as mybir
from concourse.bass_utils import run_bass_kernel_spmd

F32 = mybir.dt.float32
BF16 = mybir.dt.bfloat16
U8 = mybir.dt.uint8
AF = mybir.ActivationFunctionType
ALU = mybir.AluOpType

COMPUTE = ("pe", "act", "dve", "pool")
SCHED_SEED = 4
SCHED_JITTER = 0.003

D = 1024
NIN = 3584
DFF = 2816
NFC = DFF // 128
FF_GROUPS = [(0, 8), (8, 15), (15, 22)]
ALPHA = float(2.0 ** 0.25)
LN_EPS = 1e-5
HD = 128
L = 128


class Op:
    __slots__ = ("eng", "fn", "reads", "writes", "dma", "idx", "deps", "signal",
                 "sem", "semval", "prev_semval", "name", "dur", "nbytes", "odeps", "succ", "prio", "t_end", "boost")

    def __init__(self, eng, fn, reads, writes, dma, name, dur=0.5, nbytes=0, boost=0.0):
        self.boost = boost
        self.dur = dur
        self.nbytes = nbytes
        self.odeps = []
        self.eng = eng
        self.fn = fn
        self.reads = tuple(reads)
        self.writes = tuple(writes)
        self.dma = dma
        self.name = name
        self.deps = []
        self.signal = False
        self.sem = None
        self.semval = None
        self.prev_semval = 0


class Prog:
    def __init__(self, nc, n_dma_sems=12):
        self.nc = nc
        self.ops = []
        self.n_dma_sems = n_dma_sems
        self.groups = {}

    def buf(self, group, lo, hi):
        self.groups[group] = (lo, hi)

    def op(self, eng, fn, reads=(), writes=(), name="", dur=0.5, boost=0.0):
        o = Op(eng, fn, reads, writes, False, name, dur=dur, boost=boost)
        o.idx = len(self.ops)
        self.ops.append(o)
        return o

    def dma(self, eng, fn, reads=(), writes=(), name="", nbytes=65536, boost=0.0):
        o = Op(eng, fn, reads, writes, True, name, dur=(1.06 if eng == "pool" else 0.08), nbytes=nbytes, boost=boost)
        o.idx = len(self.ops)
        self.ops.append(o)
        return o

    def analyze(self):
        last_w = {}
        rd_eng = {}
        rd_dma = {}
        live = {}
        grp_acc = {}
        for o in self.ops:
            deps = {}
            odeps = {}

            def add(d, kind):
                if d is None or d is o:
                    return
                if not d.dma and not o.dma and d.eng == o.eng:
                    if o.eng == "pe":
                        odeps[d.idx] = d
                        return
                deps[d.idx] = d

            for r in o.reads:
                add(last_w.get(r), "raw")
            for w in o.writes:
                add(last_w.get(w), "waw")
                for d in rd_eng.get(w, ()):
                    add(d, "war")
                for d in rd_dma.get(w, ()):
                    add(d, "war")
                g = w.split(":")[0]
                iv = self.groups.get(g)
                if iv is not None:
                    for g2, acc in grp_acc.items():
                        if g2 == g:
                            continue
                        iv2 = self.groups[g2]
                        if iv2[0] < iv[1] and iv[0] < iv2[1]:
                            for d in acc["eng"]:
                                add(d, "alias")
                            for d in acc["dma"]:
                                add(d, "alias")
                    dead = [k2 for k2, g2 in live.items() if g2 != g and
                            self.groups[g2][0] < iv[1] and iv[0] < self.groups[g2][1]]
                    for k2 in dead:
                        del live[k2]
                        last_w.pop(k2, None)
                        rd_eng.pop(k2, None)
                        rd_dma.pop(k2, None)
                    live[w] = g
            for r in o.reads:
                if r.split(":")[0] in self.groups:
                    assert r in live, ("read of dead/unwritten region", r, o.name)
                if o.dma:
                    rd_dma.setdefault(r, []).append(o)
                else:
                    rd_eng.setdefault(r, []).append(o)
            for w in o.writes:
                last_w[w] = o
                rd_eng[w] = []
                rd_dma[w] = []
            for k in o.reads + o.writes:
                g = k.split(":")[0]
                if g in self.groups:
                    acc = grp_acc.setdefault(g, {"eng": [], "dma": []})
                    if o.dma:
                        if not acc["dma"] or acc["dma"][-1] is not o:
                            acc["dma"].append(o)
                    else:
                        if not acc["eng"] or acc["eng"][-1] is not o:
                            acc["eng"].append(o)
            o.deps = list(deps.values())
            o.odeps = list(odeps.values())
            for d in o.deps:
                d.signal = True

    def schedule(self, window=0.5, hop=0.1, dma_bw=300e3, dma_lat=2.0):
        ops = self.ops
        for o in ops:
            o.succ = []
        for o in ops:
            for d in o.deps + o.odeps:
                d.succ.append(o)
        for o in reversed(ops):
            o.prio = o.dur + o.boost + (o.nbytes / dma_bw if o.dma else 0.0) + max([x.prio for x in o.succ], default=0.0)
        npred = {o.idx: len(o.deps) + len(o.odeps) for o in ops}
        avail = {}
        for o in ops:
            if npred[o.idx] == 0:
                avail.setdefault(o.eng, []).append(o)
        free = {}
        dma_free = {}
        order = {}
        done = 0
        n = len(ops)
        while done < n:
            best = None
            cands = []
            for e, lst in avail.items():
                for o in lst:
                    est = free.get(e, 0.0)
                    for d in o.deps:
                        est = max(est, d.t_end + hop)
                    for d in o.odeps:
                        est = max(est, d.t_end)
                    cands.append((est, o))
            tmin = min(c[0] for c in cands)
            best = None
            for est, o in cands:
                if est <= tmin + window:
                    jit = (((o.idx * 2654435761 + SCHED_SEED * 40503) >> 7) % 1000) / 1000.0
                    key = (-o.prio * (1.0 + SCHED_JITTER * jit), est, o.idx)
                    if best is None or key < best[0]:
                        best = (key, est, o)
            _, est, o = best
            e = o.eng
            if o.dma:
                issue_end = est + o.dur
                free[e] = issue_end
                tstart = max(issue_end, dma_free.get(e, 0.0))
                dma_free[e] = tstart + o.nbytes / dma_bw
                o.t_end = dma_free[e] + dma_lat
            else:
                o.t_end = est + o.dur
                free[e] = o.t_end
            order.setdefault(e, []).append(o)
            avail[e].remove(o)
            done += 1
            for x in o.succ:
                npred[x.idx] -= 1
                if npred[x.idx] == 0:
                    avail.setdefault(x.eng, []).append(x)
        self.order = order
        self.t_est = max(o.t_end for o in ops)

    def emit(self, final_eng="sp", do_schedule=True):
        nc = self.nc
        self.analyze()
        if do_schedule:
            self.schedule()
            engs = self.order
            seq = sorted(self.ops, key=lambda o: o.t_end)
        else:
            engs = {}
            for o in self.ops:
                engs.setdefault(o.eng, []).append(o)
            seq = self.ops
        for o in self.ops:
            if o.dma:
                o.signal = True
        with contextlib.ExitStack() as st:
            csem = {e: st.enter_context(nc.semaphore("c_" + e)) for e in COMPUTE}
            dsem = {}
            for e in engs:
                if any(o.dma for o in engs[e]):
                    dsem[e] = [st.enter_context(nc.semaphore("d_%s_%d" % (e, i)))
                               for i in range(self.n_dma_sems)]
            cnt = {e: 0 for e in COMPUTE}
            duse = {e: [0] * self.n_dma_sems for e in dsem}
            drr = {e: 0 for e in dsem}
            for o in [x for e in engs for x in engs[e]]:
                if o.dma:
                    k = drr[o.eng]
                    drr[o.eng] = (k + 1) % self.n_dma_sems
                    o.sem = dsem[o.eng][k]
                    o.prev_semval = duse[o.eng][k] * 16
                    duse[o.eng][k] += 1
                    o.semval = duse[o.eng][k] * 16
                elif o.signal:
                    cnt[o.eng] += 1
                    o.sem = csem[o.eng]
                    o.semval = cnt[o.eng]
            self.stats = dict(cnt)
            block = st.enter_context(nc.Block())

            def make(ename, oplist):
                def body(eng):
                    seen = {}

                    def wait(sem, val):
                        key = id(sem)
                        if seen.get(key, 0) >= val:
                            return
                        seen[key] = val
                        eng.wait_ge(sem, val)

                    for o in oplist:
                        need = {}
                        for d in o.deps:
                            k_ = id(d.sem)
                            if k_ not in need or need[k_][1] < d.semval:
                                need[k_] = (d.sem, d.semval)
                        for sem_, val_ in need.values():
                            wait(sem_, val_)
                        if o.dma and o.prev_semval > 0:
                            wait(o.sem, o.prev_semval)
                        ins = o.fn(eng)
                        if o.signal:
                            assert ins is not None, o.name
                            ins.then_inc(o.sem, 16 if o.dma else 1)
                    if ename == final_eng:
                        for e2 in dsem:
                            for k, s in enumerate(dsem[e2]):
                                if duse[e2][k] > 0:
                                    wait(s, duse[e2][k] * 16)
                return body

            order = {"pe": block.tensor, "act": block.scalar, "dve": block.vector,
                     "pool": block.gpsimd, "sp": block.sync}
            for ename, reg in order.items():
                if ename in engs or ename == final_eng:
                    reg(make(ename, engs.get(ename, [])))


class Arena:
    def __init__(self, nc, nbytes, prog=None, name="arena"):
        self.t = nc.alloc_sbuf_tensor(name, [128, nbytes], U8).ap()
        self.off = 0
        self.nbytes = nbytes
        self.peak = 0
        self.prog = prog

    def alloc(self, shape_free, dtype, group=None, at=None):
        esz = {F32: 4, BF16: 2, U8: 1}[dtype]
        n = int(np.prod(shape_free)) * esz
        n_al = (n + 63) // 64 * 64
        if at is None:
            at = self.off
            self.off += n_al
        assert at + n <= self.nbytes, ("arena overflow", at, n, self.nbytes)
        self.peak = max(self.peak, at + n_al)
        if group is not None:
            self.prog.buf(group, at, at + n_al)
        ap = self.t[:, at:at + n].bitcast(dtype)
        if len(shape_free) > 1:
            names = " ".join("d%d" % i for i in range(len(shape_free)))
            kw = {"d%d" % i: int(s) for i, s in enumerate(shape_free)}
            ap = ap.rearrange("p (%s) -> p %s" % (names, names), **kw)
        return ap


def build(S):
    NT = S // 128
    NB = S // 512
    assert S % 512 == 0
    nc = bass.Bass("TRN2", target_bir_lowering=False)

    def din(name, shape, dt=F32):
        return nc.dram_tensor(name, list(shape), dt, kind="ExternalInput").ap()

    x_d = din("x", [S, D])
    c_d = din("c_l", [128, 8])
    adaw_d = din("ada_w", [D, 6 * D])
    adabT_d = din("ada_bT", [128, 48])
    win_d = din("w_in", [D, NIN])
    convw_d = din("convw_l", [128, 12])
    wout_d = din("w_out", [D, D])
    ln1g_d = din("ln1_g", [1, D])
    ln1b_d = din("ln1_b", [1, D])
    ln1gT_d = din("ln1_gT", [128, 8])
    ln1bT_d = din("ln1_bT", [128, 8])
    wg_d = din("w_gate", [D, DFF])
    wu_d = din("w_up", [D, DFF])
    wd_d = din("w_down", [DFF, D])
    ln2g_d = din("ln2_g", [1, D])
    ln2b_d = din("ln2_b", [1, D])
    identf_d = din("ident_f", [128, 128])
    identb_d = din("ident_b", [128, 128], BF16)
    maskT_d = din("maskT", [128, 128])
    gLt_d = din("gLt", [128, 512])
    rope_d = din("rope", [NT, 128, 1536])
    out_d = nc.dram_tensor("out", [S, D], F32, kind="ExternalOutput").ap()

    adaw_v = adaw_d.rearrange("(kc p) n -> p kc n", p=128)
    win_v = win_d.rearrange("(kc p) n -> p kc n", p=128)
    wout_v = wout_d.rearrange("(kc p) n -> p kc n", p=128)
    wg_v = wg_d.rearrange("(kc p) n -> p kc n", p=128)
    wu_v = wu_d.rearrange("(kc p) n -> p kc n", p=128)
    wd_v = wd_d.rearrange("(fc p) n -> p fc n", p=128)

    P = Prog(nc)
    A = Arena(nc, 207 * 1024, P)
    identf = A.alloc([128], F32, "identf")
    identb = A.alloc([128], BF16, "identb")
    maskT = A.alloc([128], F32, "maskT")
    gLt = A.alloc([4, 128], F32, "gLt")
    convw = A.alloc([4, 3], F32, "convw")
    c_s = A.alloc([8], F32, "c_s")
    sc_bf = A.alloc([8], BF16, "sc_bf")
    one_f = A.alloc([128], F32, "one_f")
    mhalf = A.alloc([4], F32, "mhalf")
    abT = A.alloc([48], F32, "abT")
    modT = A.alloc([32], F32, "modT")
    gcol = A.alloc([4], F32, "gcol")
    dg = [A.alloc([128], F32, "dg%d" % i) for i in range(2)]
    lnT = A.alloc([16], F32, "lnT")
    mod2 = A.alloc([16], F32, "mod2")
    G_m = A.alloc([D], F32, "Gm")
    G_f = A.alloc([D], F32, "Gf")
    lng = A.alloc([D], F32, "lng")
    lnb = A.alloc([D], F32, "lnb")
    _sv = A.off
    A.off = _sv - 2 * D * 4
    adab_late = A.alloc([8, 512], BF16, "adab_late")
    assert A.off <= _sv
    A.off = _sv
    Z0 = A.off
    wout = A.alloc([8, D], BF16, "wout")
    yT = A.alloc([8, S], BF16, "yT")
    ZB = A.off
    GMAX = max(b - a for a, b in FF_GROUPS)
    A.off = Z0
    aT = A.alloc([GMAX, S], BF16, "aT")
    wd_s = A.alloc([GMAX, D], BF16, "wd_s")
    assert A.off <= ZB, (A.off, ZB)
    A.off = ZB
    win = A.alloc([8, NIN], BF16, "win")
    W1 = A.off
    xblk = A.alloc([4, D], F32, "xblk")
    hT = [A.alloc([8, 512], BF16, "hT%d" % i) for i in range(2)]
    Cs = A.alloc([512], F32, "Cs")
    ubuf = A.alloc([514], F32, "ubuf")
    halo = A.alloc([4, 2], F32, "halo")
    yv = A.alloc([512], F32, "yv")
    rope = [A.alloc([1536], F32, "rope%d" % i) for i in range(1)]
    v_tm = [A.alloc([512], BF16, "v_tm%d" % i) for i in range(2)]
    sg = [A.alloc([512], F32, "sg%d" % i) for i in range(2)]
    rA = [A.alloc([4, 2, 64], F32, "rA%d" % i) for i in range(2)]
    rB = [A.alloc([4, 2, 64], F32, "rB%d" % i) for i in range(2)]
    qk_tm = [A.alloc([1024], BF16, "qk_tm%d" % i) for i in range(2)]
    qkT = [A.alloc([8, 128], BF16, "qkT%d" % i) for i in range(2)]
    Pm = [A.alloc([4, 128], BF16, "Pm%d" % i) for i in range(2)]
    rn = A.alloc([4, 128], F32, "rn")
    yret = A.alloc([512], BF16, "yret")
    U32 = A.alloc([4, 128], F32, "U32")
    U16 = A.alloc([4, 128], BF16, "U16")
    bn6 = A.alloc([4, 6], F32, "bn6")
    mv = A.alloc([4, 2], F32, "mv")
    ve = A.alloc([4], F32, "ve")
    rstd = A.alloc([4], F32, "rstd")
    nmr = A.alloc([4], F32, "nmr")
    wstage = A.alloc([D], F32, "wstage")
    p1_end = A.off
    A.off = W1
    NADB = 4
    adab = [A.alloc([8, 512], BF16, "adab%d" % i) for i in range(NADB)]
    A.off = ZB
    R = A.alloc([NT, D], F32, "R")
    h2T = A.alloc([8, S], BF16, "h2T")
    ZC = A.off
    xt = [A.alloc([D], F32, "xt%d" % i) for i in range(2)]
    l1 = [dict(bn6=A.alloc([2, 6], F32, "l1bn6%d" % i), mv=A.alloc([2], F32, "l1mv%d" % i), ve=A.alloc([1], F32, "l1ve%d" % i),
               rstd=A.alloc([1], F32, "l1rstd%d" % i), nmr=A.alloc([1], F32, "l1nmr%d" % i), tag="l1", i=i) for i in range(2)]
    p2_end = A.off
    wgu = [A.alloc([2, 8, 128], BF16, "wgu%d" % i) for i in range(3)]
    xn3 = [A.alloc([D], F32, "xn3%d" % i) for i in range(2)]
    sgate = [A.alloc([1024], F32, "sgate%d" % i) for i in range(2)]
    l2 = [dict(bn6=A.alloc([2, 6], F32, "l2bn6%d" % i), mv=A.alloc([2], F32, "l2mv%d" % i), ve=A.alloc([1], F32, "l2ve%d" % i),
               rstd=A.alloc([1], F32, "l2rstd%d" % i), nmr=A.alloc([1], F32, "l2nmr%d" % i), tag="l2", i=i) for i in range(2)]
    p3_end = A.off
    A.off = ZC
    wst3 = [A.alloc([D], F32, "wst3%d" % i) for i in range(2)]
    assert A.off <= p2_end, (A.off, p2_end)

    ps = [nc.alloc_psum_tensor("ps%d" % i, [128, 512], F32).ap() for i in range(8)]
    psb = [p.bitcast(BF16) for p in ps]
    sp = "sp"

    def c_mm(n, N=512):
        return n * (N / 2400.0 + 0.005)

    def c_act(N, aps=0):
        return 0.22 + N / 1200.0 + 0.1 * aps

    def c_dve(N):
        return (N + 150) / 960.0

    def c_pool(N):
        return 0.2 + N / 475.0

    P.dma(sp, lambda e: e.dma_start(out=c_s, in_=c_d), writes=["c_s:"])
    P.dma(sp, lambda e: e.dma_start(out=identf, in_=identf_d), writes=["identf:"])
    P.dma(sp, lambda e: e.dma_start(out=abT, in_=adabT_d), writes=["abT:"])
    P.dma(sp, lambda e: e.dma_start(out=lnT[:, 0:8], in_=ln1gT_d), writes=["lnT:g"])
    P.dma(sp, lambda e: e.dma_start(out=lnT[:, 8:16], in_=ln1bT_d), writes=["lnT:b"])
    P.dma(sp, lambda e: e.dma_start(out=identb, in_=identb_d), writes=["identb:"])
    P.dma(sp, lambda e: e.dma_start(out=maskT, in_=maskT_d), writes=["maskT:"])
    P.dma(sp, lambda e: e.dma_start(out=gLt.rearrange("p a b -> p (a b)"), in_=gLt_d), writes=["gLt:"])
    P.dma(sp, lambda e: e.dma_start(out=convw.rearrange("p a b -> p (a b)"), in_=convw_d), writes=["convw:"])
    P.op("pool", lambda e: e.memset(one_f, 1.0), writes=["one_f:"])
    P.op("pool", lambda e: e.memset(mhalf, -0.5), writes=["mhalf:"])
    P.op("act", lambda e: e.activation(out=sc_bf, in_=c_s, func=AF.Silu), reads=["c_s:"], writes=["sc_bf:"])

    NAB = 12
    vec_col = {0: 0, 1: 4, 2: 8, 3: 12, 6: 16, 7: 20, 8: 24, 9: 28}
    gate_dst = {4: (G_m, "Gm:", 0), 5: (G_m, "Gm:", 1), 10: (G_f, "Gf:", 0), 11: (G_f, "Gf:", 1)}
    win_dmas = [(lambda e, j=j: e.dma_start(out=win[:, :, j * 512:(j + 1) * 512], in_=win_v[:, :, j * 512:(j + 1) * 512]), "win:%d" % j)
                for j in range(7)]
    ada_order = [0, 1, 2, 3, 4, 5, 6, 7, 8, 9, 10, 11]
    ada_dma_issued = 0
    win_issued = [0]

    def issue_win(k):
        for _ in range(k):
            if win_issued[0] < len(win_dmas):
                fn, key = win_dmas[win_issued[0]]
                P.dma("pool", fn, writes=[key], nbytes=2 * 1024 * 1024,
                      boost={"win:3": 60, "win:4": 60, "win:5": 50, "win:6": 50}.get(key, 40))
                win_issued[0] += 1

    def ada_buf(ai):
        if ai < NADB:
            return adab[ai], "adab%d:" % ai
        return adab_late, "adab_late:"

    def issue_ada_dma(ai):
        blk = ada_order[ai]
        buf, key = ada_buf(ai)
        P.dma("pool", (lambda e, blk=blk, buf=buf: e.dma_start(out=buf, in_=adaw_v[:, :, blk * 512:(blk + 1) * 512])),
              writes=[key], nbytes=2 * 1024 * 1024, boost=(100 if ai < NADB else 0))

    for ai in range(NADB):
        issue_ada_dma(ai)
    issue_win(100)
    def ada_block(ai):
            blk = ada_order[ai]
            abuf, akey = ada_buf(ai)
            late = ai >= NADB
            b = (ai % 2) if not late else 4
            if late:
                issue_ada_dma(ai)

            def mm_ada(e, abuf=abuf, b=b):
                ins = None
                for c in range(4):
                    for kc in range(8):
                        ins = e.matmul(ps[b][:, c:c + 1], lhsT=abuf[:, kc, c * 128:(c + 1) * 128], rhs=sc_bf[:, kc:kc + 1],
                                       start=(kc == 0), stop=(kc == 7))
                return ins
            P.op("pe", mm_ada, reads=["sc_bf:", akey], writes=["ps%d" % b], dur=2.2)
            a0 = blk * 4
            if blk in vec_col:
                c0 = vec_col[blk]
                plus1 = 1.0 if blk in (2, 3, 8, 9) else 0.0
                P.op("dve", (lambda e, c0=c0, a0=a0, b=b, plus1=plus1: e.scalar_tensor_tensor(
                    out=modT[:, c0:c0 + 4], in0=ps[b][:, 0:4], scalar=plus1, in1=abT[:, a0:a0 + 4], op0=ALU.add, op1=ALU.add)),
                    reads=["abT:"], writes=["ps%d" % b, "modT:%d" % blk], dur=0.15)
            else:
                G, gkey, half = gate_dst[blk]
                gb = 3 if not late else 7
                P.op("dve", (lambda e, a0=a0, b=b: e.tensor_tensor(out=gcol, in0=ps[b][:, 0:4], in1=abT[:, a0:a0 + 4], op=ALU.add)),
                     reads=["abT:"], writes=["ps%d" % b, "gcol:"], dur=0.15)
                for c in range(4):
                    db = c % 2
                    P.op("dve", (lambda e, c=c, db=db: e.tensor_scalar(out=dg[db], in0=identf, scalar1=gcol[:, c:c + 1], scalar2=None, op0=ALU.mult)),
                         reads=["gcol:", "identf:"], writes=["dg%d:" % db], dur=0.3)
                    P.op("pe", (lambda e, c=c, db=db, gb=gb: e.matmul(ps[gb][:, c * 128:(c + 1) * 128], lhsT=one_f, rhs=dg[db], start=True, stop=True)),
                         reads=["dg%d:" % db, "one_f:"], writes=["ps%d" % gb], dur=0.3)
                P.op("dve", (lambda e, G=G, half=half, gb=gb: e.tensor_copy(out=G[:, half * 512:(half + 1) * 512], in_=ps[gb])),
                     writes=["ps%d" % gb, gkey + "%d" % half], dur=0.7)

    for ai in range(NADB):
        ada_block(ai)
    MODT = ["modT:%d" % b_ for b_ in (0, 1, 2, 3)]
    MODF = ["modT:%d" % b_ for b_ in (6, 7, 8, 9)]
    P.op("dve", lambda e: e.memset(U32, 0.0), writes=["U32:"], dur=0.7)
    P.op("dve", lambda e: e.memset(halo, 0.0), writes=["halo:"], dur=0.1)
    HT = [["hT%d:%d" % (hb, dc) for dc in range(8)] for hb in range(2)]
    FB = 7
    XT_BOOST = 0.0

    def x_load(tb):
        for t in range(4):
            n = tb * 4 + t
            P.dma(sp, (lambda e, t=t, n=n: e.dma_start(out=xblk[:, t, :], in_=x_d[n * 128:(n + 1) * 128, :])),
                  writes=["xblk:%d" % t], nbytes=512 * 1024)

    def x_transpose(tb, dc, bk):
        hb = tb % 2

        def tr(e):
            ins = None
            for t in range(4):
                ins = e.transpose(ps[bk][:, t * 128:(t + 1) * 128], xblk[:, t, dc * 128:(dc + 1) * 128], identf)
            return ins
        P.op("pe", tr, reads=["xblk:%d" % t for t in range(4)] + ["identf:"], writes=["ps%d" % bk], dur=0.95)
        P.op("act", lambda e: e.activation(out=hT[hb][:, dc, :], in_=ps[bk], func=AF.Identity,
                                          bias=modT[:, dc:dc + 1], scale=modT[:, 8 + dc:9 + dc]),
             reads=MODT, writes=["ps%d" % bk, HT[hb][dc]], dur=c_act(512, 2), boost=XT_BOOST)

    def proj_fm(e, bank, wcol0, hb):
        ins = None
        for kc in range(8):
            ins = e.matmul(bank, lhsT=win[:, kc, wcol0:wcol0 + 128], rhs=hT[hb][:, kc, :], start=(kc == 0), stop=(kc == 7))
        return ins

    def conv(tb, cc):
        hb = tb % 2
        FBk = "ps%d" % FB
        P.op("pe", lambda e: proj_fm(e, ps[FB], cc * 128, hb), reads=HT[hb] + ["win:0"], writes=[FBk], dur=c_mm(8))
        P.op("act", lambda e: e.activation(out=Cs, in_=ps[FB], func=AF.Copy), writes=[FBk, "Cs:"], dur=c_act(512))
        P.op("pe", lambda e: proj_fm(e, ps[4], 512 + cc * 128, hb), reads=HT[hb] + ["win:1"], writes=["ps4"], dur=c_mm(8))
        P.op("dve", lambda e: e.tensor_copy(out=ubuf[:, 0:2], in_=halo[:, cc, :]), reads=["halo:"], writes=["ubuf:h"], dur=0.1)
        P.op("dve", lambda e: e.tensor_tensor(out=ubuf[:, 2:514], in0=ps[4], in1=Cs, op=ALU.mult),
             reads=["Cs:"], writes=["ps4", "ubuf:b"], dur=c_dve(512))
        P.op("dve", lambda e: e.tensor_scalar(out=yv, in0=ubuf[:, 2:514], scalar1=convw[:, cc, 2:3], scalar2=None, op0=ALU.mult),
             reads=["ubuf:b", "convw:"], writes=["yv:"], dur=0.95)
        P.op("dve", lambda e: e.scalar_tensor_tensor(out=yv, in0=ubuf[:, 1:513], scalar=convw[:, cc, 1:2], in1=yv,
                                                    op0=ALU.mult, op1=ALU.add),
             reads=["ubuf:b", "ubuf:h", "convw:", "yv:"], writes=["yv:"], dur=0.75)
        P.op("dve", lambda e: e.scalar_tensor_tensor(out=yv, in0=ubuf[:, 0:512], scalar=convw[:, cc, 0:1], in1=yv,
                                                    op0=ALU.mult, op1=ALU.add),
             reads=["ubuf:b", "ubuf:h", "convw:", "yv:"], writes=["yv:"], dur=0.75)
        P.op("dve", lambda e: e.tensor_copy(out=halo[:, cc, :], in_=ubuf[:, 512:514]), reads=["ubuf:b"], writes=["halo:"], dur=0.1)
        P.op("pe", lambda e: proj_fm(e, ps[FB], 1024 + cc * 128, hb), reads=HT[hb] + ["win:2"], writes=[FBk], dur=c_mm(8))
        P.op("dve", lambda e: e.tensor_tensor(out=yT[:, cc, tb * 512:(tb + 1) * 512], in0=ps[FB], in1=yv, op=ALU.mult),
             reads=["yv:"], writes=[FBk, "yT:c%d_%d" % (cc, tb)], dur=c_dve(512))

    def _proj(e, n, j0):
        tb, t = divmod(n, 4)
        hb = tb % 2
        ins = None
        for kc in range(8):
            for j in (j0, j0 + 1):
                ins = e.matmul(ps[j], lhsT=hT[hb][:, kc, t * 128:(t + 1) * 128],
                               rhs=win[:, kc, 1536 + j * 512: 1536 + (j + 1) * 512], start=(kc == 0), stop=(kc == 7))
        return ins

    def tile(n):
        tb, t = divmod(n, 4)
        hb = tb % 2
        pb = n % 2
        QK = ["qk_tm%d:0" % pb, "qk_tm%d:1" % pb]
        qk_t, qkT_, Pm_, v_t, sg_ = qk_tm[pb], qkT[pb], Pm[pb], v_tm[pb], sg[pb]
        RP = "rope0:"
        P.dma(sp, lambda e: e.dma_start(out=rope[0], in_=rope_d[n]), writes=[RP], nbytes=768 * 1024)
        P.op("pe", lambda e: _proj(e, n, 0), reads=HT[hb] + ["win:3", "win:4"], writes=["ps0", "ps1"], dur=c_mm(16))
        for qi, (bk, c0, s0) in enumerate(((0, 0, 256), (1, 768, 1024))):
            src = ps[bk].rearrange("p (h t f) -> p h t f", h=4, t=2)
            ctab = rope[0][:, c0:c0 + 256].rearrange("p (h f) -> p h f", h=4).unsqueeze(2).to_broadcast([128, 4, 2, 64])
            stab = rope[0][:, s0:s0 + 512].rearrange("p (h t f) -> p h t f", h=4, t=2)
            P.op("dve", (lambda e, src=src, ctab=ctab, qi=qi: e.tensor_tensor(out=rA[qi], in0=src, in1=ctab, op=ALU.mult)),
                 reads=[RP], writes=["ps%d" % bk, "rA%d:" % qi], dur=c_dve(512))
            P.op("dve", (lambda e, src=src, stab=stab, qi=qi: e.tensor_tensor(out=rB[qi], in0=src[:, :, ::-1, :], in1=stab, op=ALU.mult)),
                 reads=[RP], writes=["ps%d" % bk, "rB%d:" % qi], dur=c_dve(512))
            P.op("dve", (lambda e, qi=qi: e.tensor_tensor(out=qk_t[:, qi * 512:(qi + 1) * 512],
                                                         in0=rA[qi].rearrange("p a b c -> p (a b c)"),
                                                         in1=rB[qi].rearrange("p a b c -> p (a b c)"), op=ALU.add)),
                 reads=["rA%d:" % qi, "rB%d:" % qi], writes=[QK[qi]], dur=c_dve(512))
        P.op("pe", lambda e: _proj(e, n, 2), reads=HT[hb] + ["win:5", "win:6"], writes=["ps2", "ps3"], dur=c_mm(16))
        P.op("act", lambda e: e.activation(out=v_t, in_=ps[2], func=AF.Copy), writes=["ps2", "v_tm%d:" % pb], dur=c_act(512))
        P.op("act", lambda e: e.activation(out=sg_, in_=ps[3], func=AF.Silu), writes=["ps3", "sg%d:" % pb], dur=c_act(512))

        def tr_qk(e):
            ins = None
            for j in range(8):
                ins = e.transpose(psb[5][:, j * 128:(j + 1) * 128], qk_t[:, j * 128:(j + 1) * 128], identb)
            return ins
        P.op("pe", tr_qk, reads=QK + ["identb:"], writes=["ps5"], dur=0.6)
        P.op("act", lambda e: e.activation(out=qkT_.rearrange("p a b -> p (a b)"), in_=psb[5], func=AF.Copy),
             writes=["ps5", "qkT%d:" % pb], dur=c_act(1024))

        def mm_S(e):
            ins = None
            for h in range(4):
                ins = e.matmul(ps[5][:, h * 128:(h + 1) * 128], lhsT=qkT_[:, 4 + h, :], rhs=qkT_[:, h, :], start=True, stop=True)
            return ins
        P.op("pe", mm_S, reads=["qkT%d:" % pb], writes=["ps5"], dur=0.3)
        P.op("dve", lambda e: e.tensor_tensor(out=Pm_, in0=ps[5].rearrange("p (h i) -> p h i", h=4),
                                             in1=maskT.unsqueeze(1).to_broadcast([128, 4, 128]), op=ALU.mult),
             reads=["maskT:"], writes=["ps5", "Pm%d:" % pb], dur=c_dve(512))

        def mm_r(e):
            ins = None
            for h in range(4):
                o = ps[6][:, h * 128:(h + 1) * 128]
                ins = e.matmul(o, lhsT=Pm_[:, h, :], rhs=v_t[:, h * 128:(h + 1) * 128], start=True, stop=(n == 0))
                if n > 0:
                    ins = e.matmul(o, lhsT=qkT_[:, h, :], rhs=U16[:, h, :], start=False, stop=True)
            return ins
        P.op("pe", mm_r, reads=["Pm%d:" % pb, "v_tm%d:" % pb, "qkT%d:" % pb] + (["U16:"] if n > 0 else []), writes=["ps6"], dur=0.55)
        if n < NT - 1:
            def mm_T(e):
                ins = None
                for h in range(4):
                    ins = e.matmul(ps[5][:, h * 128:(h + 1) * 128], lhsT=qk_t[:, 512 + h * 128: 512 + (h + 1) * 128],
                                   rhs=v_t[:, h * 128:(h + 1) * 128], start=True, stop=True)
                return ins
            P.op("pe", mm_T, reads=[QK[1], "v_tm%d:" % pb], writes=["ps5"], dur=0.3)
            P.op("dve", lambda e: e.tensor_tensor(out=U32, in0=ps[5].rearrange("p (h i) -> p h i", h=4), in1=U32, op=ALU.add),
                 reads=["U32:"], writes=["ps5", "U32:"], dur=c_dve(512))
            P.op("dve", lambda e: e.tensor_tensor(out=U32, in0=U32, in1=gLt, op=ALU.mult),
                 reads=["U32:", "gLt:"], writes=["U32:"], dur=c_dve(512))
            P.op("act", lambda e: e.activation(out=U16, in_=U32, func=AF.Copy), reads=["U32:"], writes=["U16:"], dur=c_act(512))
        for h in range(4):
            P.op("dve", (lambda e, h=h: e.bn_stats(out=bn6[:, h, :], in_=ps[6][:, h * 128:(h + 1) * 128])),
                 writes=["ps6", "bn6:%d" % h], dur=0.32)
        for h in range(4):
            P.op("dve", (lambda e, h=h: e.bn_aggr(out=mv[:, h, :], in_=bn6[:, h, :])), reads=["bn6:%d" % h], writes=["mv:%d" % h], dur=0.08)
        MV = ["mv:%d" % h for h in range(4)]
        P.op("pool", lambda e: e.tensor_scalar(out=ve, in0=mv[:, :, 1], scalar1=LN_EPS, scalar2=None, op0=ALU.add),
             reads=MV, writes=["ve:"], dur=0.3)
        P.op("pool", lambda e: e.tensor_tensor(out=rstd, in0=ve, in1=mhalf, op=ALU.pow), reads=["ve:", "mhalf:"], writes=["rstd:"], dur=1.0)
        P.op("dve", lambda e: e.scalar_tensor_tensor(out=nmr, in0=mv[:, :, 0], scalar=-1.0, in1=rstd, op0=ALU.mult, op1=ALU.mult),
             reads=MV + ["rstd:"], writes=["nmr:"], dur=0.1)
        for h in range(4):
            P.op("act", (lambda e, h=h: e.activation(out=rn[:, h, :], in_=ps[6][:, h * 128:(h + 1) * 128], func=AF.Identity,
                                                    bias=nmr[:, h:h + 1], scale=rstd[:, h:h + 1])),
                 reads=["rstd:", "nmr:"], writes=["ps6", "rn:%d" % h], dur=c_act(128, 2))
        P.op("dve", lambda e: e.tensor_tensor(out=yret, in0=rn.rearrange("p a b -> p (a b)"), in1=sg_, op=ALU.mult),
             reads=["rn:%d" % h for h in range(4)] + ["sg%d:" % pb], writes=["yret:"], dur=c_dve(512))

        def tr_y(e):
            ins = None
            for h in range(4):
                ins = e.transpose(psb[5][:, h * 128:(h + 1) * 128], yret[:, h * 128:(h + 1) * 128], identb)
            return ins
        P.op("pe", tr_y, reads=["yret:", "identb:"], writes=["ps5"], dur=0.3)
        P.op("act", lambda e: e.activation(out=yT[:, 4:8, n * 128:(n + 1) * 128],
                                          in_=psb[5][:, 0:512].rearrange("p (h i) -> p h i", h=4), func=AF.Copy),
             writes=["ps5", "yT:r%d" % n], dur=c_act(512))

    def wout_fold(kc):
        P.dma(sp, lambda e: e.dma_start(out=wstage, in_=wout_d[kc * 128:(kc + 1) * 128, :]), writes=["wstage:"], nbytes=512 * 1024)
        P.op("pool", lambda e: e.tensor_tensor(out=wout[:, kc, :], in0=wstage, in1=G_m, op=ALU.mult),
             reads=["wstage:", "Gm:0", "Gm:1"], writes=["wout:%d" % kc], dur=c_pool(1024))

    for tb in range(NB):
        x_load(tb)
        for dc in range(8):
            x_transpose(tb, dc, ([7, 5, 3, 4][dc % 4] if tb == 0 else (4 if dc % 2 == 0 else 7)))
        for t in range(4):
            tile(tb * 4 + t)
            conv(tb, t)
            if NADB + tb * 4 + t < NAB:
                ada_block(NADB + tb * 4 + t)
            if 2 <= tb * 4 + t < 10 and NT >= 10:
                wout_fold(tb * 4 + t - 2)
    if NT < 10:
        for kc in range(8):
            wout_fold(kc)

    P.dma(sp, lambda e: e.dma_start(out=lng, in_=ln1g_d.partition_broadcast(128)[:, 0, :]), writes=["lng:"])
    P.dma(sp, lambda e: e.dma_start(out=lnb, in_=ln1b_d.partition_broadcast(128)[:, 0, :]), writes=["lnb:"])

    P.op("dve", lambda e: e.tensor_tensor(out=mod2[:, 0:8], in0=lnT[:, 0:8], in1=modT[:, 24:32], op=ALU.mult),
         reads=["lnT:g"] + MODF, writes=["mod2:s"], dur=0.1)
    P.op("dve", lambda e: e.tensor_tensor(out=mod2[:, 8:16], in0=lnT[:, 8:16], in1=modT[:, 24:32], op=ALU.mult),
         reads=["lnT:b"] + MODF, writes=["mod2:b"], dur=0.1)
    P.op("dve", lambda e: e.tensor_tensor(out=mod2[:, 8:16], in0=mod2[:, 8:16], in1=modT[:, 16:24], op=ALU.add),
         reads=["mod2:b"] + MODF, writes=["mod2:b"], dur=0.1)

    def ln_tile(n, L_, xnx, xkey, dst, dst_key, affine=True):
        tag, i = L_["tag"], L_["i"]
        k = lambda s_: "%s%s%d:" % (tag, s_, i)
        src_key = "R:%d" % n
        for half in range(2):
            P.op("dve", (lambda e, half=half: e.bn_stats(out=L_["bn6"][:, half, :], in_=R[:, n, half * 512:(half + 1) * 512])),
                 reads=[src_key], writes=["%sbn6%d:%d" % (tag, i, half)], dur=0.69)
        P.op("dve", lambda e: e.bn_aggr(out=L_["mv"], in_=L_["bn6"].rearrange("p a b -> p (a b)")),
             reads=["%sbn6%d:0" % (tag, i), "%sbn6%d:1" % (tag, i)], writes=[k("mv")], dur=0.2)
        P.op("pool", lambda e: e.tensor_scalar(out=L_["ve"], in0=L_["mv"][:, 1:2], scalar1=LN_EPS, scalar2=None, op0=ALU.add),
             reads=[k("mv")], writes=[k("ve")], dur=0.25)
        P.op("pool", lambda e: e.tensor_tensor(out=L_["rstd"], in0=L_["ve"], in1=mhalf[:, 0:1], op=ALU.pow),
             reads=[k("ve"), "mhalf:"], writes=[k("rstd")], dur=0.55)
        P.op("dve", lambda e: e.scalar_tensor_tensor(out=L_["nmr"], in0=L_["mv"][:, 0:1], scalar=-1.0, in1=L_["rstd"], op0=ALU.mult, op1=ALU.mult),
             reads=[k("mv"), k("rstd")], writes=[k("nmr")], dur=0.1)
        P.op("act", lambda e: e.activation(out=xnx, in_=R[:, n, :], func=AF.Identity, bias=L_["nmr"][:, 0:1], scale=L_["rstd"][:, 0:1]),
             reads=[src_key, k("rstd"), k("nmr")], writes=[xkey], dur=c_act(1024, 2))
        if affine:
            P.op("dve", lambda e: e.tensor_tensor(out=xnx, in0=xnx, in1=lng, op=ALU.mult), reads=[xkey, "lng:"], writes=[xkey], dur=c_dve(1024))
            P.op("pool", lambda e: e.tensor_tensor(out=dst, in0=xnx, in1=lnb, op=ALU.add), reads=[xkey, "lnb:"], writes=[dst_key], dur=c_pool(1024))

    def p2_tile(n):
        xb = n % 2
        tb = n // 4
        pb = 2 * (n % 2)
        P.dma(sp, lambda e: e.dma_start(out=xt[xb], in_=x_d[n * 128:(n + 1) * 128, :]), writes=["xt%d:" % xb], nbytes=512 * 1024)

        def mm_out(e):
            ins = None
            for half in range(2):
                for kc in range(8):
                    ins = e.matmul(ps[pb + half], lhsT=yT[:, kc, n * 128:(n + 1) * 128], rhs=wout[:, kc, half * 512:(half + 1) * 512],
                                   start=(kc == 0), stop=(kc == 7))
            return ins
        P.op("pe", mm_out, reads=["yT:c%d_%d" % (cc, tb) for cc in range(4)] + ["yT:r%d" % n] + ["wout:%d" % kc for kc in range(8)],
             writes=["ps%d" % pb, "ps%d" % (pb + 1)], dur=c_mm(16))
        for half in range(2):
            hs = slice(half * 512, (half + 1) * 512)
            P.op("dve", (lambda e, half=half, hs=hs: e.scalar_tensor_tensor(out=R[:, n, hs], in0=xt[xb][:, hs], scalar=ALPHA, in1=ps[pb + half],
                                                                         op0=ALU.mult, op1=ALU.add)),
                 reads=["xt%d:" % xb], writes=["ps%d" % (pb + half), "R:%d" % n], dur=c_dve(512))
        ln_tile(n, l1[xb], R[:, n, :], "R:%d" % n, None, None, affine=False)

    def p2_h2T(tb):
        for dc in range(8):
            bk = 4 + dc % 4

            def tr2(e, dc=dc, bk=bk):
                ins = None
                for t in range(4):
                    ins = e.transpose(ps[bk][:, t * 128:(t + 1) * 128], R[:, tb * 4 + t, dc * 128:(dc + 1) * 128], identf)
                return ins
            P.op("pe", tr2, reads=["R:%d" % (tb * 4 + t) for t in range(4)] + ["identf:"], writes=["ps%d" % bk], dur=0.95)
            P.op("act", (lambda e, dc=dc, bk=bk: e.activation(
                out=h2T[:, dc, tb * 512:(tb + 1) * 512], in_=ps[bk], func=AF.Identity,
                bias=mod2[:, 8 + dc:9 + dc], scale=mod2[:, dc:dc + 1])),
                reads=["mod2:s", "mod2:b"], writes=["ps%d" % bk, "h2T:%d_%d" % (dc, tb)], dur=c_act(512, 2))

    for n in range(NT):
        p2_tile(n)
        if n % 4 == 3:
            p2_h2T(n // 4)

    H2T = ["h2T:%d_%d" % (dc, tb) for dc in range(8) for tb in range(NB)]
    NH = S // 1024 if S >= 1024 else 1
    HW = S // NH
    NBH = HW // 512
    q = 0
    sgi = 0
    for gi, (f0, f1) in enumerate(FF_GROUPS):
        ng = f1 - f0
        last = (gi == len(FF_GROUPS) - 1)
        if gi == 1:
            P.dma(sp, lambda e: e.dma_start(out=lng, in_=ln2g_d.partition_broadcast(128)[:, 0, :]), writes=["lng:"], nbytes=512 * 1024)
            P.dma(sp, lambda e: e.dma_start(out=lnb, in_=ln2b_d.partition_broadcast(128)[:, 0, :]), writes=["lnb:"], nbytes=512 * 1024)
        for fi in range(ng):
            wsb = (f0 + fi) % 2
            P.dma(sp, (lambda e, fi=fi, f0=f0, wsb=wsb: e.dma_start(out=wst3[wsb], in_=wd_d[(f0 + fi) * 128:(f0 + fi + 1) * 128, :])),
                  writes=["wst3%d:" % wsb], nbytes=512 * 1024)
            P.op("dve", (lambda e, fi=fi, wsb=wsb: e.tensor_tensor(out=wd_s[:, fi, :], in0=wst3[wsb], in1=G_f, op=ALU.mult)),
                 reads=["wst3%d:" % wsb, "Gf:0", "Gf:1"], writes=["wd_s:%d" % fi], dur=c_dve(1024))
        def load_wgu(fc, wb):
            P.dma("pool", lambda e: e.dma_start(out=wgu[wb][:, 0, :, :], in_=wg_v[:, :, fc * 128:(fc + 1) * 128]),
                  writes=["wgu%d:g" % wb], nbytes=512 * 1024)
            P.dma("pool", lambda e: e.dma_start(out=wgu[wb][:, 1, :, :], in_=wu_v[:, :, fc * 128:(fc + 1) * 128]),
                  writes=["wgu%d:u" % wb], nbytes=512 * 1024)

        def gate_up(fi, hf, wb, sb_, ng=ng):
            pb = 4 * (hf % 2)

            def mm_gu(e):
                ins = None
                for gu in range(2):
                    for kc in range(8):
                        for b in range(NBH):
                            t0 = hf * HW + b * 512
                            ins = e.matmul(ps[pb + gu * 2 + b], lhsT=wgu[wb][:, gu, kc, :], rhs=h2T[:, kc, t0:t0 + 512],
                                           start=(kc == 0), stop=(kc == 7))
                return ins
            banks = ["ps%d" % (pb + gu * 2 + b) for gu in range(2) for b in range(NBH)]
            h2r = ["h2T:%d_%d" % (dc, (hf * HW) // 512 + b) for dc in range(8) for b in range(NBH)]
            P.op("pe", mm_gu, reads=h2r + ["wgu%d:g" % wb, "wgu%d:u" % wb], writes=banks, dur=c_mm(16 * NBH))
            for b in range(NBH):
                t0 = hf * HW + b * 512
                P.op("act", (lambda e, b=b: e.activation(out=sgate[sb_][:, b * 512:(b + 1) * 512], in_=ps[pb + b], func=AF.Silu)),
                     writes=["ps%d" % (pb + b), "sgate%d:%d" % (sb_, b)], dur=c_act(512))
                P.op("dve", (lambda e, b=b, t0=t0: e.tensor_tensor(
                    out=aT[:, fi, t0:t0 + 512], in0=ps[pb + 2 + b], in1=sgate[sb_][:, b * 512:(b + 1) * 512], op=ALU.mult)),
                    reads=["sgate%d:%d" % (sb_, b)], writes=["ps%d" % (pb + 2 + b), "aT:%d_%d" % (fi, t0 // 512)], dur=c_dve(512))

        def down(n, nbanks, gi=gi, ng=ng, last=last):
            pb = 2 * (n % nbanks)
            xb = n % 2

            def mm_dn(e):
                ins = None
                for half in range(2):
                    for fi in range(ng):
                        ins = e.matmul(ps[pb + half], lhsT=aT[:, fi, n * 128:(n + 1) * 128], rhs=wd_s[:, fi, half * 512:(half + 1) * 512],
                                       start=(fi == 0), stop=(fi == ng - 1))
                return ins
            P.op("pe", mm_dn, reads=["aT:%d_%d" % (fi, n // 4) for fi in range(ng)] + ["wd_s:%d" % fi for fi in range(ng)],
                 writes=["ps%d" % pb, "ps%d" % (pb + 1)], dur=c_mm(2 * ng))
            for half in range(2):
                hs = slice(half * 512, (half + 1) * 512)
                if gi == 0:
                    P.op("dve", (lambda e, half=half, hs=hs: e.scalar_tensor_tensor(
                        out=R[:, n, hs], in0=R[:, n, hs], scalar=ALPHA, in1=ps[pb + half], op0=ALU.mult, op1=ALU.add)),
                        reads=["R:%d" % n], writes=["ps%d" % (pb + half), "R:%d" % n], dur=c_dve(512))
                else:
                    P.op("dve", (lambda e, half=half, hs=hs: e.tensor_tensor(
                        out=R[:, n, hs], in0=ps[pb + half], in1=R[:, n, hs], op=ALU.add)),
                        reads=["R:%d" % n], writes=["ps%d" % (pb + half), "R:%d" % n], dur=c_dve(512))
            if last:
                ln_tile(n, l2[xb], xn3[xb], "xn3%d:" % xb, xn3[xb], "xn3%d:" % xb)
                P.dma(sp, lambda e: e.dma_start(out=out_d[n * 128:(n + 1) * 128, :], in_=xn3[xb]),
                      reads=["xn3%d:" % xb], writes=["out%d" % n], nbytes=512 * 1024)

        if not last or NH == 1:
            for fi in range(ng):
                wb = q % 3
                q += 1
                load_wgu(f0 + fi, wb)
                for hf in range(NH):
                    gate_up(fi, hf, wb, sgi % 2)
                    sgi += 1
                if gi == 0 and fi < 4:
                    for n in range(fi * NT // 4, (fi + 1) * NT // 4):
                        dep = ["aT:%d_%d" % (fi, b_) for b_ in range(NB)]
                        P.op("dve", (lambda e, n=n: e.tensor_tensor(out=R[:, n, :], in0=R[:, n, :], in1=lng, op=ALU.mult)),
                             reads=["R:%d" % n, "lng:"] + dep, writes=["R:%d" % n], dur=c_dve(1024))
                        P.op("pool", (lambda e, n=n: e.tensor_tensor(out=R[:, n, :], in0=R[:, n, :], in1=lnb, op=ALU.add)),
                             reads=["R:%d" % n, "lnb:"], writes=["R:%d" % n], dur=c_pool(1024))
            for n in range(NT):
                down(n, 4)
        else:
            TPH = NT // NH
            for hf in range(NH):
                for fi in range(ng):
                    wb = q % 3
                    q += 1
                    load_wgu(f0 + fi, wb)
                    gate_up(fi, hf, wb, sgi % 2)
                    sgi += 1
                for n in range(hf * TPH, (hf + 1) * TPH):
                    down(n, 2 if hf + 1 < NH else 4)

    P.emit()
    nc._P = P
    nc._prog_stats = (P.stats, len(P.ops), A.peak, p1_end, p2_end, p3_end, getattr(P, "t_est", None))
    return nc


def _constants(S):
    NT = S // 128
    ident_f = np.eye(128, dtype=np.float32)
    ident_b = np.eye(128, dtype=np.float32).astype(ml_dtypes.bfloat16)
    j = np.arange(128)[:, None]
    i = np.arange(128)[None, :]
    maskT = (i >= j).astype(np.float32)
    h = np.arange(4, dtype=np.float64)
    log_g = np.log1p(-(2.0 ** (-5.0 - h)))
    gL = np.exp(log_g * L)
    gLt = np.broadcast_to(np.repeat(gL, 128)[None, :], (128, 512)).astype(np.float32).copy()
    inv_freq = (10000.0 ** (-np.arange(0, HD, 2, dtype=np.float32) / np.float32(HD))).astype(np.float32)
    pos = np.arange(S, dtype=np.float32)
    ang = (pos[:, None] * inv_freq[None, :]).astype(np.float32)
    cos = np.cos(ang.astype(np.float64))
    sin = np.sin(ang.astype(np.float64))
    ii = (np.arange(S) % L).astype(np.float64)
    dq = np.exp(log_g[None, :] * ii[:, None])
    dk = np.exp(-log_g[None, :] * ii[:, None]) * (HD ** -0.5)
    rope = np.zeros((S, 1536), np.float64)
    cq = dq[:, :, None] * cos[:, None, :]
    sq = dq[:, :, None] * sin[:, None, :]
    ck = dk[:, :, None] * cos[:, None, :]
    sk = dk[:, :, None] * sin[:, None, :]
    rope[:, 0:256] = cq.reshape(S, 256)
    rope[:, 256:768] = np.stack([-sq, sq], axis=2).reshape(S, 512)
    rope[:, 768:1024] = ck.reshape(S, 256)
    rope[:, 1024:1536] = np.stack([-sk, sk], axis=2).reshape(S, 512)
    rope = rope.astype(np.float32).reshape(NT, 128, 1536)
    return dict(ident_f=ident_f, ident_b=ident_b, maskT=maskT, gLt=gLt, rope=rope)


_NC_CACHE = {}


def make_in_maps(S, x, c, ada_w, ada_b, w_in, conv_w, w_out, ln1_g, ln1_b, w_gate, w_up, w_down, ln2_g, ln2_b):
    B = x.shape[0]
    cst = _constants(S)
    f = lambda a: np.ascontiguousarray(np.asarray(a, dtype=np.float32))
    shared = dict(
        ada_w=f(ada_w[0]), ada_bT=f(np.asarray(ada_b[0]).reshape(48, 128).T),
        w_in=f(w_in[0]),
        convw_l=f(np.asarray(conv_w[0]).reshape(3, 4, 128).transpose(2, 1, 0).reshape(128, 12)),
        w_out=f(w_out[0]), ln1_g=f(ln1_g[0]).reshape(1, -1), ln1_b=f(ln1_b[0]).reshape(1, -1),
        ln1_gT=f(np.asarray(ln1_g[0]).reshape(8, 128).T), ln1_bT=f(np.asarray(ln1_b[0]).reshape(8, 128).T),
        w_gate=f(w_gate[0]), w_up=f(w_up[0]), w_down=f(w_down[0]),
        ln2_g=f(ln2_g[0]).reshape(1, -1), ln2_b=f(ln2_b[0]).reshape(1, -1), **cst)
    in_maps = []
    for b in range(B):
        m = dict(shared)
        m["x"] = f(x[b])
        m["c_l"] = f(np.asarray(c[b]).reshape(8, 128).T)
        in_maps.append(m)
    return in_maps


def kernel(x, c, ada_w, ada_b, w_in, conv_w, w_out, ln1_g, ln1_b, w_gate, w_up, w_down, ln2_g, ln2_b):
    x = np.asarray(x)
    B, S, _ = x.shape
    in_maps = make_in_maps(S, x, c, ada_w, ada_b, w_in, conv_w, w_out, ln1_g, ln1_b, w_gate, w_up, w_down, ln2_g, ln2_b)
    if S not in _NC_CACHE:
        _NC_CACHE[S] = build(S)
    nc = _NC_CACHE[S]
    res = run_bass_kernel_spmd(nc, in_maps, core_ids=list(range(B)))
    return np.stack([np.asarray(r["out"], dtype=np.float32) for r in res.results], axis=0)
```

```python
import contextlib
import numpy as np
import ml_dtypes
import concourse.bass as bass
import concourse.mybir as mybir
from concourse.bass_utils import run_bass_kernel_spmd

F32 = mybir.dt.float32
BF16 = mybir.dt.bfloat16
U8 = mybir.dt.uint8
AF = mybir.ActivationFunctionType
ALU = mybir.AluOpType

COMPUTE = ("pe", "act", "dve", "pool")
SCHED_SEED = 5
SCHED_JITTER = 0.003

D = 1024
NIN = 3584
DFF = 2816
NFC = DFF // 128
FF_GROUPS = [(0, 8), (8, 15), (15, 22)]
ALPHA = float(2.0 ** 0.25)
LN_EPS = 1e-5
HD = 128
L = 128


class Op:
    __slots__ = ("eng", "fn", "reads", "writes", "dma", "idx", "deps", "signal",
                 "sem", "semval", "prev_semval", "name", "dur", "nbytes", "odeps", "succ", "prio", "t_end", "boost")

    def __init__(self, eng, fn, reads, writes, dma, name, dur=0.5, nbytes=0, boost=0.0):
        self.boost = boost
        self.dur = dur
        self.nbytes = nbytes
        self.odeps = []
        self.eng = eng
        self.fn = fn
        self.reads = tuple(reads)
        self.writes = tuple(writes)
        self.dma = dma
        self.name = name
        self.deps = []
        self.signal = False
        self.sem = None
        self.semval = None
        self.prev_semval = 0


class Prog:
    def __init__(self, nc, n_dma_sems=12):
        self.nc = nc
        self.ops = []
        self.n_dma_sems = n_dma_sems
        self.groups = {}

    def buf(self, group, lo, hi):
        self.groups[group] = (lo, hi)

    def op(self, eng, fn, reads=(), writes=(), name="", dur=0.5, boost=0.0):
        o = Op(eng, fn, reads, writes, False, name, dur=dur, boost=boost)
        o.idx = len(self.ops)
        self.ops.append(o)
        return o

    def dma(self, eng, fn, reads=(), writes=(), name="", nbytes=65536, boost=0.0):
        o = Op(eng, fn, reads, writes, True, name, dur=(1.06 if eng == "pool" else 0.08), nbytes=nbytes, boost=boost)
        o.idx = len(self.ops)
        self.ops.append(o)
        return o

    def analyze(self):
        last_w = {}
        rd_eng = {}
        rd_dma = {}
        live = {}
        grp_acc = {}
        for o in self.ops:
            deps = {}
            odeps = {}

            def add(d, kind):
                if d is None or d is o:
                    return
                if not d.dma and not o.dma and d.eng == o.eng:
                    if o.eng == "pe":
                        odeps[d.idx] = d
                        return
                deps[d.idx] = d

            for r in o.reads:
                add(last_w.get(r), "raw")
            for w in o.writes:
                add(last_w.get(w), "waw")
                for d in rd_eng.get(w, ()):
                    add(d, "war")
                for d in rd_dma.get(w, ()):
                    add(d, "war")
                g = w.split(":")[0]
                iv = self.groups.get(g)
                if iv is not None:
                    for g2, acc in grp_acc.items():
                        if g2 == g:
                            continue
                        iv2 = self.groups[g2]
                        if iv2[0] < iv[1] and iv[0] < iv2[1]:
                            for d in acc["eng"]:
                                add(d, "alias")
                            for d in acc["dma"]:
                                add(d, "alias")
                    dead = [k2 for k2, g2 in live.items() if g2 != g and
                            self.groups[g2][0] < iv[1] and iv[0] < self.groups[g2][1]]
                    for k2 in dead:
                        del live[k2]
                        last_w.pop(k2, None)
                        rd_eng.pop(k2, None)
                        rd_dma.pop(k2, None)
                    live[w] = g
            for r in o.reads:
                if r.split(":")[0] in self.groups:
                    assert r in live, ("read of dead/unwritten region", r, o.name)
                if o.dma:
                    rd_dma.setdefault(r, []).append(o)
                else:
                    rd_eng.setdefault(r, []).append(o)
            for w in o.writes:
                last_w[w] = o
                rd_eng[w] = []
                rd_dma[w] = []
            for k in o.reads + o.writes:
                g = k.split(":")[0]
                if g in self.groups:
                    acc = grp_acc.setdefault(g, {"eng": [], "dma": []})
                    if o.dma:
                        if not acc["dma"] or acc["dma"][-1] is not o:
                            acc["dma"].append(o)
                    else:
                        if not acc["eng"] or acc["eng"][-1] is not o:
                            acc["eng"].append(o)
            o.deps = list(deps.values())
            o.odeps = list(odeps.values())
            for d in o.deps:
                d.signal = True

    def schedule(self, window=0.5, hop=0.1, dma_bw=300e3, dma_lat=2.0):
        ops = self.ops
        for o in ops:
            o.succ = []
        for o in ops:
            for d in o.deps + o.odeps:
                d.succ.append(o)
        for o in reversed(ops):
            o.prio = o.dur + o.boost + (o.nbytes / dma_bw if o.dma else 0.0) + max([x.prio for x in o.succ], default=0.0)
        npred = {o.idx: len(o.deps) + len(o.odeps) for o in ops}
        avail = {}
        for o in ops:
            if npred[o.idx] == 0:
                avail.setdefault(o.eng, []).append(o)
        free = {}
        dma_free = {}
        order = {}
        done = 0
        n = len(ops)
        while done < n:
            best = None
            cands = []
            for e, lst in avail.items():
                for o in lst:
                    est = free.get(e, 0.0)
                    for d in o.deps:
                        est = max(est, d.t_end + hop)
                    for d in o.odeps:
                        est = max(est, d.t_end)
                    cands.append((est, o))
            tmin = min(c[0] for c in cands)
            best = None
            for est, o in cands:
                if est <= tmin + window:
                    jit = (((o.idx * 2654435761 + SCHED_SEED * 40503) >> 7) % 1000) / 1000.0
                    key = (-o.prio * (1.0 + SCHED_JITTER * jit), est, o.idx)
                    if best is None or key < best[0]:
                        best = (key, est, o)
            _, est, o = best
            e = o.eng
            if o.dma:
                issue_end = est + o.dur
                free[e] = issue_end
                tstart = max(issue_end, dma_free.get(e, 0.0))
                dma_free[e] = tstart + o.nbytes / dma_bw
                o.t_end = dma_free[e] + dma_lat
            else:
                o.t_end = est + o.dur
                free[e] = o.t_end
            order.setdefault(e, []).append(o)
            avail[e].remove(o)
            done += 1
            for x in o.succ:
                npred[x.idx] -= 1
                if npred[x.idx] == 0:
                    avail.setdefault(x.eng, []).append(x)
        self.order = order
        self.t_est = max(o.t_end for o in ops)

    def emit(self, final_eng="sp", do_schedule=True):
        nc = self.nc
        self.analyze()
        if do_schedule:
            self.schedule()
            engs = self.order
            seq = sorted(self.ops, key=lambda o: o.t_end)
        else:
            engs = {}
            for o in self.ops:
                engs.setdefault(o.eng, []).append(o)
            seq = self.ops
        for o in self.ops:
            if o.dma:
                o.signal = True
        with contextlib.ExitStack() as st:
            csem = {e: st.enter_context(nc.semaphore("c_" + e)) for e in COMPUTE}
            dsem = {}
            for e in engs:
                if any(o.dma for o in engs[e]):
                    dsem[e] = [st.enter_context(nc.semaphore("d_%s_%d" % (e, i)))
                               for i in range(self.n_dma_sems)]
            cnt = {e: 0 for e in COMPUTE}
            duse = {e: [0] * self.n_dma_sems for e in dsem}
            drr = {e: 0 for e in dsem}
            for o in [x for e in engs for x in engs[e]]:
                if o.dma:
                    k = drr[o.eng]
                    drr[o.eng] = (k + 1) % self.n_dma_sems
                    o.sem = dsem[o.eng][k]
                    o.prev_semval = duse[o.eng][k] * 16
                    duse[o.eng][k] += 1
                    o.semval = duse[o.eng][k] * 16
                elif o.signal:
                    cnt[o.eng] += 1
                    o.sem = csem[o.eng]
                    o.semval = cnt[o.eng]
            self.stats = dict(cnt)
            block = st.enter_context(nc.Block())

            def make(ename, oplist):
                def body(eng):
                    seen = {}

                    def wait(sem, val):
                        key = id(sem)
                        if seen.get(key, 0) >= val:
                            return
                        seen[key] = val
                        eng.wait_ge(sem, val)

                    for o in oplist:
                        need = {}
                        for d in o.deps:
                            k_ = id(d.sem)
                            if k_ not in need or need[k_][1] < d.semval:
                                need[k_] = (d.sem, d.semval)
                        for sem_, val_ in need.values():
                            wait(sem_, val_)
                        if o.dma and o.prev_semval > 0:
                            wait(o.sem, o.prev_semval)
                        ins = o.fn(eng)
                        if o.signal:
                            assert ins is not None, o.name
                            ins.then_inc(o.sem, 16 if o.dma else 1)
                    if ename == final_eng:
                        for e2 in dsem:
                            for k, s in enumerate(dsem[e2]):
                                if duse[e2][k] > 0:
                                    wait(s, duse[e2][k] * 16)
                return body

            order = {"pe": block.tensor, "act": block.scalar, "dve": block.vector,
                     "pool": block.gpsimd, "sp": block.sync}
            for ename, reg in order.items():
                if ename in engs or ename == final_eng:
                    reg(make(ename, engs.get(ename, [])))


class Arena:
    def __init__(self, nc, nbytes, prog=None, name="arena"):
        self.t = nc.alloc_sbuf_tensor(name, [128, nbytes], U8).ap()
        self.off = 0
        self.nbytes = nbytes
        self.peak = 0
        self.prog = prog

    def alloc(self, shape_free, dtype, group=None, at=None):
        esz = {F32: 4, BF16: 2, U8: 1}[dtype]
        n = int(np.prod(shape_free)) * esz
        n_al = (n + 63) // 64 * 64
        if at is None:
            at = self.off
            self.off += n_al
        assert at + n <= self.nbytes, ("arena overflow", at, n, self.nbytes)
        self.peak = max(self.peak, at + n_al)
        if group is not None:
            self.prog.buf(group, at, at + n_al)
        ap = self.t[:, at:at + n].bitcast(dtype)
        if len(shape_free) > 1:
            names = " ".join("d%d" % i for i in range(len(shape_free)))
            kw = {"d%d" % i: int(s) for i, s in enumerate(shape_free)}
            ap = ap.rearrange("p (%s) -> p %s" % (names, names), **kw)
        return ap


def build(S):
    NT = S // 128
    NB = S // 512
    assert S % 512 == 0
    nc = bass.Bass("TRN2", target_bir_lowering=False)

    def din(name, shape, dt=F32):
        return nc.dram_tensor(name, list(shape), dt, kind="ExternalInput").ap()

    x_d = din("x", [S, D])
    c_d = din("c_l", [128, 8])
    adaw_d = din("ada_w", [D, 6 * D])
    adabT_d = din("ada_bT", [128, 48])
    win_d = din("w_in", [D, NIN])
    convw_d = din("convw_l", [128, 12])
    wout_d = din("w_out", [D, D])
    ln1g_d = din("ln1_g", [1, D])
    ln1b_d = din("ln1_b", [1, D])
    ln1gT_d = din("ln1_gT", [128, 8])
    ln1bT_d = din("ln1_bT", [128, 8])
    wg_d = din("w_gate", [D, DFF])
    wu_d = din("w_up", [D, DFF])
    wd_d = din("w_down", [DFF, D])
    ln2g_d = din("ln2_g", [1, D])
    ln2b_d = din("ln2_b", [1, D])
    identf_d = din("ident_f", [128, 128])
    identb_d = din("ident_b", [128, 128], BF16)
    maskT_d = din("maskT", [128, 128])
    gLt_d = din("gLt", [128, 512])
    rope_d = din("rope", [NT, 128, 1536])
    out_d = nc.dram_tensor("out", [S, D], F32, kind="ExternalOutput").ap()

    adaw_v = adaw_d.rearrange("(kc p) n -> p kc n", p=128)
    win_v = win_d.rearrange("(kc p) n -> p kc n", p=128)
    wout_v = wout_d.rearrange("(kc p) n -> p kc n", p=128)
    wg_v = wg_d.rearrange("(kc p) n -> p kc n", p=128)
    wu_v = wu_d.rearrange("(kc p) n -> p kc n", p=128)
    wd_v = wd_d.rearrange("(fc p) n -> p fc n", p=128)

    P = Prog(nc)
    A = Arena(nc, 207 * 1024, P)
    identf = A.alloc([128], F32, "identf")
    identb = A.alloc([128], BF16, "identb")
    maskT = A.alloc([128], F32, "maskT")
    gLt = A.alloc([4, 128], F32, "gLt")
    convw = A.alloc([4, 3], F32, "convw")
    c_s = A.alloc([8], F32, "c_s")
    sc_bf = A.alloc([8], BF16, "sc_bf")
    one_f = A.alloc([128], F32, "one_f")
    mhalf = A.alloc([4], F32, "mhalf")
    abT = A.alloc([48], F32, "abT")
    modT = A.alloc([32], F32, "modT")
    gcol = A.alloc([4], F32, "gcol")
    dg = [A.alloc([128], F32, "dg%d" % i) for i in range(2)]
    lnT = A.alloc([16], F32, "lnT")
    mod2 = A.alloc([16], F32, "mod2")
    G_m = A.alloc([D], F32, "Gm")
    G_f = A.alloc([D], F32, "Gf")
    lng = A.alloc([D], F32, "lng")
    lnb = A.alloc([D], F32, "lnb")
    _sv = A.off
    A.off = _sv - 2 * D * 4
    adab_late = A.alloc([8, 512], BF16, "adab_late")
    assert A.off <= _sv
    A.off = _sv
    Z0 = A.off
    wout = A.alloc([8, D], BF16, "wout")
    yT = A.alloc([8, S], BF16, "yT")
    ZB = A.off
    GMAX = max(b - a for a, b in FF_GROUPS)
    A.off = Z0
    aT = A.alloc([GMAX, S], BF16, "aT")
    wd_s = A.alloc([GMAX, D], BF16, "wd_s")
    assert A.off <= ZB, (A.off, ZB)
    A.off = ZB
    win = A.alloc([8, NIN], BF16, "win")
    W1 = A.off
    xblk = A.alloc([4, D], F32, "xblk")
    hT = [A.alloc([8, 512], BF16, "hT%d" % i) for i in range(2)]
    Cs = A.alloc([512], F32, "Cs")
    ubuf = A.alloc([514], F32, "ubuf")
    halo = A.alloc([4, 2], F32, "halo")
    yv = A.alloc([512], F32, "yv")
    rope = [A.alloc([1536], F32, "rope%d" % i) for i in range(1)]
    v_tm = [A.alloc([512], BF16, "v_tm%d" % i) for i in range(2)]
    sg = [A.alloc([512], F32, "sg%d" % i) for i in range(2)]
    rA = [A.alloc([4, 2, 64], F32, "rA%d" % i) for i in range(2)]
    rB = [A.alloc([4, 2, 64], F32, "rB%d" % i) for i in range(2)]
    qk_tm = [A.alloc([1024], BF16, "qk_tm%d" % i) for i in range(2)]
    qkT = [A.alloc([8, 128], BF16, "qkT%d" % i) for i in range(2)]
    Pm = [A.alloc([4, 128], BF16, "Pm%d" % i) for i in range(2)]
    rn = A.alloc([4, 128], F32, "rn")
    yret = A.alloc([512], BF16, "yret")
    U32 = A.alloc([4, 128], F32, "U32")
    U16 = A.alloc([4, 128], BF16, "U16")
    bn6 = A.alloc([4, 6], F32, "bn6")
    mv = A.alloc([4, 2], F32, "mv")
    ve = A.alloc([4], F32, "ve")
    rstd = A.alloc([4], F32, "rstd")
    nmr = A.alloc([4], F32, "nmr")
    wstage = A.alloc([D], F32, "wstage")
    p1_end = A.off
    A.off = W1
    NADB = 4
    adab = [A.alloc([8, 512], BF16, "adab%d" % i) for i in range(NADB)]
    A.off = ZB
    R = A.alloc([NT, D], F32, "R")
    h2T = A.alloc([8, S], BF16, "h2T")
    ZC = A.off
    xt = [A.alloc([D], F32, "xt%d" % i) for i in range(2)]
    l1 = [dict(bn6=A.alloc([2, 6], F32, "l1bn6%d" % i), mv=A.alloc([2], F32, "l1mv%d" % i), ve=A.alloc([1], F32, "l1ve%d" % i),
               rstd=A.alloc([1], F32, "l1rstd%d" % i), nmr=A.alloc([1], F32, "l1nmr%d" % i), tag="l1", i=i) for i in range(2)]
    p2_end = A.off
    wgu = [A.alloc([2, 8, 128], BF16, "wgu%d" % i) for i in range(3)]
    xn3 = [A.alloc([D], F32, "xn3%d" % i) for i in range(2)]
    sgate = [A.alloc([1024], F32, "sgate%d" % i) for i in range(2)]
    l2 = [dict(bn6=A.alloc([2, 6], F32, "l2bn6%d" % i), mv=A.alloc([2], F32, "l2mv%d" % i), ve=A.alloc([1], F32, "l2ve%d" % i),
               rstd=A.alloc([1], F32, "l2rstd%d" % i), nmr=A.alloc([1], F32, "l2nmr%d" % i), tag="l2", i=i) for i in range(2)]
    p3_end = A.off
    A.off = ZC
    wst3 = [A.alloc([D], F32, "wst3%d" % i) for i in range(2)]
    assert A.off <= p2_end, (A.off, p2_end)

    ps = [nc.alloc_psum_tensor("ps%d" % i, [128, 512], F32).ap() for i in range(8)]
    psb = [p.bitcast(BF16) for p in ps]
    sp = "sp"

    def c_mm(n, N=512):
        return n * (N / 2400.0 + 0.005)

    def c_act(N, aps=0):
        return 0.22 + N / 1200.0 + 0.1 * aps

    def c_dve(N):
        return (N + 150) / 960.0

    def c_pool(N):
        return 0.2 + N / 475.0

    P.dma(sp, lambda e: e.dma_start(out=c_s, in_=c_d), writes=["c_s:"])
    P.dma(sp, lambda e: e.dma_start(out=identf, in_=identf_d), writes=["identf:"])
    P.dma(sp, lambda e: e.dma_start(out=abT, in_=adabT_d), writes=["abT:"])
    P.dma(sp, lambda e: e.dma_start(out=lnT[:, 0:8], in_=ln1gT_d), writes=["lnT:g"])
    P.dma(sp, lambda e: e.dma_start(out=lnT[:, 8:16], in_=ln1bT_d), writes=["lnT:b"])
    P.dma(sp, lambda e: e.dma_start(out=identb, in_=identb_d), writes=["identb:"])
    P.dma(sp, lambda e: e.dma_start(out=maskT, in_=maskT_d), writes=["maskT:"])
    P.dma(sp, lambda e: e.dma_start(out=gLt.rearrange("p a b -> p (a b)"), in_=gLt_d), writes=["gLt:"])
    P.dma(sp, lambda e: e.dma_start(out=convw.rearrange("p a b -> p (a b)"), in_=convw_d), writes=["convw:"])
    P.op("pool", lambda e: e.memset(one_f, 1.0), writes=["one_f:"])
    P.op("pool", lambda e: e.memset(mhalf, -0.5), writes=["mhalf:"])
    P.op("act", lambda e: e.activation(out=sc_bf, in_=c_s, func=AF.Silu), reads=["c_s:"], writes=["sc_bf:"])

    NAB = 12
    vec_col = {0: 0, 1: 4, 2: 8, 3: 12, 6: 16, 7: 20, 8: 24, 9: 28}
    gate_dst = {4: (G_m, "Gm:", 0), 5: (G_m, "Gm:", 1), 10: (G_f, "Gf:", 0), 11: (G_f, "Gf:", 1)}
    win_dmas = [(lambda e, j=j: e.dma_start(out=win[:, :, j * 512:(j + 1) * 512], in_=win_v[:, :, j * 512:(j + 1) * 512]), "win:%d" % j)
                for j in range(7)]
    ada_order = [0, 1, 2, 3, 4, 5, 6, 7, 8, 9, 10, 11]
    ada_dma_issued = 0
    win_issued = [0]

    def issue_win(k):
        for _ in range(k):
            if win_issued[0] < len(win_dmas):
                fn, key = win_dmas[win_issued[0]]
                P.dma("pool", fn, writes=[key], nbytes=2 * 1024 * 1024,
                      boost={"win:3": 60, "win:4": 60, "win:5": 50, "win:6": 50}.get(key, 40))
                win_issued[0] += 1

    def ada_buf(ai):
        if ai < NADB:
            return adab[ai], "adab%d:" % ai
        return adab_late, "adab_late:"

    def issue_ada_dma(ai):
        blk = ada_order[ai]
        buf, key = ada_buf(ai)
        P.dma("pool", (lambda e, blk=blk, buf=buf: e.dma_start(out=buf, in_=adaw_v[:, :, blk * 512:(blk + 1) * 512])),
              writes=[key], nbytes=2 * 1024 * 1024, boost=(100 if ai < NADB else 0))

    for ai in range(NADB):
        issue_ada_dma(ai)
    issue_win(100)
    def ada_block(ai):
            blk = ada_order[ai]
            abuf, akey = ada_buf(ai)
            late = ai >= NADB
            b = (ai % 2) if not late else 4
            if late:
                issue_ada_dma(ai)

            def mm_ada(e, abuf=abuf, b=b):
                ins = None
                for c in range(4):
                    for kc in range(8):
                        ins = e.matmul(ps[b][:, c:c + 1], lhsT=abuf[:, kc, c * 128:(c + 1) * 128], rhs=sc_bf[:, kc:kc + 1],
                                       start=(kc == 0), stop=(kc == 7))
                return ins
            P.op("pe", mm_ada, reads=["sc_bf:", akey], writes=["ps%d" % b], dur=2.2)
            a0 = blk * 4
            if blk in vec_col:
                c0 = vec_col[blk]
                plus1 = 1.0 if blk in (2, 3, 8, 9) else 0.0
                P.op("dve", (lambda e, c0=c0, a0=a0, b=b, plus1=plus1: e.scalar_tensor_tensor(
                    out=modT[:, c0:c0 + 4], in0=ps[b][:, 0:4], scalar=plus1, in1=abT[:, a0:a0 + 4], op0=ALU.add, op1=ALU.add)),
                    reads=["abT:"], writes=["ps%d" % b, "modT:%d" % blk], dur=0.15)
            else:
                G, gkey, half = gate_dst[blk]
                gb = 3 if not late else 7
                P.op("dve", (lambda e, a0=a0, b=b: e.tensor_tensor(out=gcol, in0=ps[b][:, 0:4], in1=abT[:, a0:a0 + 4], op=ALU.add)),
                     reads=["abT:"], writes=["ps%d" % b, "gcol:"], dur=0.15)
                for c in range(4):
                    db = c % 2
                    P.op("dve", (lambda e, c=c, db=db: e.tensor_scalar(out=dg[db], in0=identf, scalar1=gcol[:, c:c + 1], scalar2=None, op0=ALU.mult)),
                         reads=["gcol:", "identf:"], writes=["dg%d:" % db], dur=0.3)
                    P.op("pe", (lambda e, c=c, db=db, gb=gb: e.matmul(ps[gb][:, c * 128:(c + 1) * 128], lhsT=one_f, rhs=dg[db], start=True, stop=True)),
                         reads=["dg%d:" % db, "one_f:"], writes=["ps%d" % gb], dur=0.3)
                P.op("dve", (lambda e, G=G, half=half, gb=gb: e.tensor_copy(out=G[:, half * 512:(half + 1) * 512], in_=ps[gb])),
                     writes=["ps%d" % gb, gkey + "%d" % half], dur=0.7)

    for ai in range(NADB):
        ada_block(ai)
    MODT = ["modT:%d" % b_ for b_ in (0, 1, 2, 3)]
    MODF = ["modT:%d" % b_ for b_ in (6, 7, 8, 9)]
    P.op("dve", lambda e: e.memset(U32, 0.0), writes=["U32:"], dur=0.7)
    P.op("dve", lambda e: e.memset(halo, 0.0), writes=["halo:"], dur=0.1)
    HT = [["hT%d:%d" % (hb, dc) for dc in range(8)] for hb in range(2)]
    FB = 7
    XT_BOOST = 0.0

    def x_load(tb):
        for t in range(4):
            n = tb * 4 + t
            P.dma(sp, (lambda e, t=t, n=n: e.dma_start(out=xblk[:, t, :], in_=x_d[n * 128:(n + 1) * 128, :])),
                  writes=["xblk:%d" % t], nbytes=512 * 1024)

    def x_transpose(tb, dc, bk):
        hb = tb % 2

        def tr(e):
            ins = None
            for t in range(4):
                ins = e.transpose(ps[bk][:, t * 128:(t + 1) * 128], xblk[:, t, dc * 128:(dc + 1) * 128], identf)
            return ins
        P.op("pe", tr, reads=["xblk:%d" % t for t in range(4)] + ["identf:"], writes=["ps%d" % bk], dur=0.95)
        P.op("act", lambda e: e.activation(out=hT[hb][:, dc, :], in_=ps[bk], func=AF.Identity,
                                          bias=modT[:, dc:dc + 1], scale=modT[:, 8 + dc:9 + dc]),
             reads=MODT, writes=["ps%d" % bk, HT[hb][dc]], dur=c_act(512, 2), boost=XT_BOOST)

    def proj_fm(e, bank, wcol0, hb):
        ins = None
        for kc in range(8):
            ins = e.matmul(bank, lhsT=win[:, kc, wcol0:wcol0 + 128], rhs=hT[hb][:, kc, :], start=(kc == 0), stop=(kc == 7))
        return ins

    def conv(tb, cc):
        hb = tb % 2
        FBk = "ps%d" % FB
        P.op("pe", lambda e: proj_fm(e, ps[FB], cc * 128, hb), reads=HT[hb] + ["win:0"], writes=[FBk], dur=c_mm(8))
        P.op("act", lambda e: e.activation(out=Cs, in_=ps[FB], func=AF.Copy), writes=[FBk, "Cs:"], dur=c_act(512))
        P.op("pe", lambda e: proj_fm(e, ps[4], 512 + cc * 128, hb), reads=HT[hb] + ["win:1"], writes=["ps4"], dur=c_mm(8))
        P.op("dve", lambda e: e.tensor_copy(out=ubuf[:, 0:2], in_=halo[:, cc, :]), reads=["halo:"], writes=["ubuf:h"], dur=0.1)
        P.op("dve", lambda e: e.tensor_tensor(out=ubuf[:, 2:514], in0=ps[4], in1=Cs, op=ALU.mult),
             reads=["Cs:"], writes=["ps4", "ubuf:b"], dur=c_dve(512))
        P.op("dve", lambda e: e.tensor_scalar(out=yv, in0=ubuf[:, 2:514], scalar1=convw[:, cc, 2:3], scalar2=None, op0=ALU.mult),
             reads=["ubuf:b", "convw:"], writes=["yv:"], dur=0.95)
        P.op("dve", lambda e: e.scalar_tensor_tensor(out=yv, in0=ubuf[:, 1:513], scalar=convw[:, cc, 1:2], in1=yv,
                                                    op0=ALU.mult, op1=ALU.add),
             reads=["ubuf:b", "ubuf:h", "convw:", "yv:"], writes=["yv:"], dur=0.75)
        P.op("dve", lambda e: e.scalar_tensor_tensor(out=yv, in0=ubuf[:, 0:512], scalar=convw[:, cc, 0:1], in1=yv,
                                                    op0=ALU.mult, op1=ALU.add),
             reads=["ubuf:b", "ubuf:h", "convw:", "yv:"], writes=["yv:"], dur=0.75)
        P.op("dve", lambda e: e.tensor_copy(out=halo[:, cc, :], in_=ubuf[:, 512:514]), reads=["ubuf:b"], writes=["halo:"], dur=0.1)
        P.op("pe", lambda e: proj_fm(e, ps[FB], 1024 + cc * 128, hb), reads=HT[hb] + ["win:2"], writes=[FBk], dur=c_mm(8))
        P.op("dve", lambda e: e.tensor_tensor(out=yT[:, cc, tb * 512:(tb + 1) * 512], in0=ps[FB], in1=yv, op=ALU.mult),
             reads=["yv:"], writes=[FBk, "yT:c%d_%d" % (cc, tb)], dur=c_dve(512))

    def _proj(e, n, j0):
        tb, t = divmod(n, 4)
        hb = tb % 2
        ins = None
        for kc in range(8):
            for j in (j0, j0 + 1):
                ins = e.matmul(ps[j], lhsT=hT[hb][:, kc, t * 128:(t + 1) * 128],
                               rhs=win[:, kc, 1536 + j * 512: 1536 + (j + 1) * 512], start=(kc == 0), stop=(kc == 7))
        return ins

    def tile(n):
        tb, t = divmod(n, 4)
        hb = tb % 2
        pb = n % 2
        QK = ["qk_tm%d:0" % pb, "qk_tm%d:1" % pb]
        qk_t, qkT_, Pm_, v_t, sg_ = qk_tm[pb], qkT[pb], Pm[pb], v_tm[pb], sg[pb]
        RP = "rope0:"
        P.dma(sp, lambda e: e.dma_start(out=rope[0], in_=rope_d[n]), writes=[RP], nbytes=768 * 1024)
        P.op("pe", lambda e: _proj(e, n, 0), reads=HT[hb] + ["win:3", "win:4"], writes=["ps0", "ps1"], dur=c_mm(16))
        for qi, (bk, c0, s0) in enumerate(((0, 0, 256), (1, 768, 1024))):
            src = ps[bk].rearrange("p (h t f) -> p h t f", h=4, t=2)
            ctab = rope[0][:, c0:c0 + 256].rearrange("p (h f) -> p h f", h=4).unsqueeze(2).to_broadcast([128, 4, 2, 64])
            stab = rope[0][:, s0:s0 + 512].rearrange("p (h t f) -> p h t f", h=4, t=2)
            P.op("dve", (lambda e, src=src, ctab=ctab, qi=qi: e.tensor_tensor(out=rA[qi], in0=src, in1=ctab, op=ALU.mult)),
                 reads=[RP], writes=["ps%d" % bk, "rA%d:" % qi], dur=c_dve(512))
            P.op("dve", (lambda e, src=src, stab=stab, qi=qi: e.tensor_tensor(out=rB[qi], in0=src[:, :, ::-1, :], in1=stab, op=ALU.mult)),
                 reads=[RP], writes=["ps%d" % bk, "rB%d:" % qi], dur=c_dve(512))
            P.op("dve", (lambda e, qi=qi: e.tensor_tensor(out=qk_t[:, qi * 512:(qi + 1) * 512],
                                                         in0=rA[qi].rearrange("p a b c -> p (a b c)"),
                                                         in1=rB[qi].rearrange("p a b c -> p (a b c)"), op=ALU.add)),
                 reads=["rA%d:" % qi, "rB%d:" % qi], writes=[QK[qi]], dur=c_dve(512))
        P.op("pe", lambda e: _proj(e, n, 2), reads=HT[hb] + ["win:5", "win:6"], writes=["ps2", "ps3"], dur=c_mm(16))
        P.op("act", lambda e: e.activation(out=v_t, in_=ps[2], func=AF.Copy), writes=["ps2", "v_tm%d:" % pb], dur=c_act(512))
        P.op("act", lambda e: e.activation(out=sg_, in_=ps[3], func=AF.Silu), writes=["ps3", "sg%d:" % pb], dur=c_act(512))

        def tr_qk(e):
            ins = None
            for j in range(8):
                ins = e.transpose(psb[5][:, j * 128:(j + 1) * 128], qk_t[:, j * 128:(j + 1) * 128], identb)
            return ins
        P.op("pe", tr_qk, reads=QK + ["identb:"], writes=["ps5"], dur=0.6)
        P.op("act", lambda e: e.activation(out=qkT_.rearrange("p a b -> p (a b)"), in_=psb[5], func=AF.Copy),
             writes=["ps5", "qkT%d:" % pb], dur=c_act(1024))

        def mm_S(e):
            ins = None
            for h in range(4):
                ins = e.matmul(ps[5][:, h * 128:(h + 1) * 128], lhsT=qkT_[:, 4 + h, :], rhs=qkT_[:, h, :], start=True, stop=True)
            return ins
        P.op("pe", mm_S, reads=["qkT%d:" % pb], writes=["ps5"], dur=0.3)
        P.op("dve", lambda e: e.tensor_tensor(out=Pm_, in0=ps[5].rearrange("p (h i) -> p h i", h=4),
                                             in1=maskT.unsqueeze(1).to_broadcast([128, 4, 128]), op=ALU.mult),
             reads=["maskT:"], writes=["ps5", "Pm%d:" % pb], dur=c_dve(512))

        def mm_r(e):
            ins = None
            for h in range(4):
                o = ps[6][:, h * 128:(h + 1) * 128]
                ins = e.matmul(o, lhsT=Pm_[:, h, :], rhs=v_t[:, h * 128:(h + 1) * 128], start=True, stop=(n == 0))
                if n > 0:
                    ins = e.matmul(o, lhsT=qkT_[:, h, :], rhs=U16[:, h, :], start=False, stop=True)
            return ins
        P.op("pe", mm_r, reads=["Pm%d:" % pb, "v_tm%d:" % pb, "qkT%d:" % pb] + (["U16:"] if n > 0 else []), writes=["ps6"], dur=0.55)
        if n < NT - 1:
            def mm_T(e):
                ins = None
                for h in range(4):
                    ins = e.matmul(ps[5][:, h * 128:(h + 1) * 128], lhsT=qk_t[:, 512 + h * 128: 512 + (h + 1) * 128],
                                   rhs=v_t[:, h * 128:(h + 1) * 128], start=True, stop=True)
                return ins
            P.op("pe", mm_T, reads=[QK[1], "v_tm%d:" % pb], writes=["ps5"], dur=0.3)
            P.op("dve", lambda e: e.tensor_tensor(out=U32, in0=ps[5].rearrange("p (h i) -> p h i", h=4), in1=U32, op=ALU.add),
                 reads=["U32:"], writes=["ps5", "U32:"], dur=c_dve(512))
            P.op("dve", lambda e: e.tensor_tensor(out=U32, in0=U32, in1=gLt, op=ALU.mult),
                 reads=["U32:", "gLt:"], writes=["U32:"], dur=c_dve(512))
            P.op("act", lambda e: e.activation(out=U16, in_=U32, func=AF.Copy), reads=["U32:"], writes=["U16:"], dur=c_act(512))
        for h in range(4):
            P.op("dve", (lambda e, h=h: e.bn_stats(out=bn6[:, h, :], in_=ps[6][:, h * 128:(h + 1) * 128])),
                 writes=["ps6", "bn6:%d" % h], dur=0.32)
        for h in range(4):
            P.op("dve", (lambda e, h=h: e.bn_aggr(out=mv[:, h, :], in_=bn6[:, h, :])), reads=["bn6:%d" % h], writes=["mv:%d" % h], dur=0.08)
        MV = ["mv:%d" % h for h in range(4)]
        P.op("pool", lambda e: e.tensor_scalar(out=ve, in0=mv[:, :, 1], scalar1=LN_EPS, scalar2=None, op0=ALU.add),
             reads=MV, writes=["ve:"], dur=0.3)
        P.op("pool", lambda e: e.tensor_tensor(out=rstd, in0=ve, in1=mhalf, op=ALU.pow), reads=["ve:", "mhalf:"], writes=["rstd:"], dur=1.0)
        P.op("dve", lambda e: e.scalar_tensor_tensor(out=nmr, in0=mv[:, :, 0], scalar=-1.0, in1=rstd, op0=ALU.mult, op1=ALU.mult),
             reads=MV + ["rstd:"], writes=["nmr:"], dur=0.1)
        for h in range(4):
            P.op("act", (lambda e, h=h: e.activation(out=rn[:, h, :], in_=ps[6][:, h * 128:(h + 1) * 128], func=AF.Identity,
                                                    bias=nmr[:, h:h + 1], scale=rstd[:, h:h + 1])),
                 reads=["rstd:", "nmr:"], writes=["ps6", "rn:%d" % h], dur=c_act(128, 2))
        P.op("dve", lambda e: e.tensor_tensor(out=yret, in0=rn.rearrange("p a b -> p (a b)"), in1=sg_, op=ALU.mult),
             reads=["rn:%d" % h for h in range(4)] + ["sg%d:" % pb], writes=["yret:"], dur=c_dve(512))

        def tr_y(e):
            ins = None
            for h in range(4):
                ins = e.transpose(psb[5][:, h * 128:(h + 1) * 128], yret[:, h * 128:(h + 1) * 128], identb)
            return ins
        P.op("pe", tr_y, reads=["yret:", "identb:"], writes=["ps5"], dur=0.3)
        P.op("act", lambda e: e.activation(out=yT[:, 4:8, n * 128:(n + 1) * 128],
                                          in_=psb[5][:, 0:512].rearrange("p (h i) -> p h i", h=4), func=AF.Copy),
             writes=["ps5", "yT:r%d" % n], dur=c_act(512))

    def wout_fold(kc):
        P.dma(sp, lambda e: e.dma_start(out=wstage, in_=wout_d[kc * 128:(kc + 1) * 128, :]), writes=["wstage:"], nbytes=512 * 1024)
        P.op("pool", lambda e: e.tensor_tensor(out=wout[:, kc, :], in0=wstage, in1=G_m, op=ALU.mult),
             reads=["wstage:", "Gm:0", "Gm:1"], writes=["wout:%d" % kc], dur=c_pool(1024))

    for tb in range(NB):
        x_load(tb)
        for dc in range(8):
            x_transpose(tb, dc, ([7, 5, 3, 4][dc % 4] if tb == 0 else (4 if dc % 2 == 0 else 7)))
        for t in range(4):
            tile(tb * 4 + t)
            conv(tb, t)
            if NADB + tb * 4 + t < NAB:
                ada_block(NADB + tb * 4 + t)
            if 2 <= tb * 4 + t < 10 and NT >= 10:
                wout_fold(tb * 4 + t - 2)
    if NT < 10:
        for kc in range(8):
            wout_fold(kc)

    P.dma(sp, lambda e: e.dma_start(out=lng, in_=ln1g_d.partition_broadcast(128)[:, 0, :]), writes=["lng:"])
    P.dma(sp, lambda e: e.dma_start(out=lnb, in_=ln1b_d.partition_broadcast(128)[:, 0, :]), writes=["lnb:"])

    P.op("dve", lambda e: e.tensor_tensor(out=mod2[:, 0:8], in0=lnT[:, 0:8], in1=modT[:, 24:32], op=ALU.mult),
         reads=["lnT:g"] + MODF, writes=["mod2:s"], dur=0.1)
    P.op("dve", lambda e: e.tensor_tensor(out=mod2[:, 8:16], in0=lnT[:, 8:16], in1=modT[:, 24:32], op=ALU.mult),
         reads=["lnT:b"] + MODF, writes=["mod2:b"], dur=0.1)
    P.op("dve", lambda e: e.tensor_tensor(out=mod2[:, 8:16], in0=mod2[:, 8:16], in1=modT[:, 16:24], op=ALU.add),
         reads=["mod2:b"] + MODF, writes=["mod2:b"], dur=0.1)

    def ln_tile(n, L_, xnx, xkey, dst, dst_key, affine=True):
        tag, i = L_["tag"], L_["i"]
        k = lambda s_: "%s%s%d:" % (tag, s_, i)
        src_key = "R:%d" % n
        for half in range(2):
            P.op("dve", (lambda e, half=half: e.bn_stats(out=L_["bn6"][:, half, :], in_=R[:, n, half * 512:(half + 1) * 512])),
                 reads=[src_key], writes=["%sbn6%d:%d" % (tag, i, half)], dur=0.69)
        P.op("dve", lambda e: e.bn_aggr(out=L_["mv"], in_=L_["bn6"].rearrange("p a b -> p (a b)")),
             reads=["%sbn6%d:0" % (tag, i), "%sbn6%d:1" % (tag, i)], writes=[k("mv")], dur=0.2)
        P.op("pool", lambda e: e.tensor_scalar(out=L_["ve"], in0=L_["mv"][:, 1:2], scalar1=LN_EPS, scalar2=None, op0=ALU.add),
             reads=[k("mv")], writes=[k("ve")], dur=0.25)
        P.op("pool", lambda e: e.tensor_tensor(out=L_["rstd"], in0=L_["ve"], in1=mhalf[:, 0:1], op=ALU.pow),
             reads=[k("ve"), "mhalf:"], writes=[k("rstd")], dur=0.55)
        P.op("dve", lambda e: e.scalar_tensor_tensor(out=L_["nmr"], in0=L_["mv"][:, 0:1], scalar=-1.0, in1=L_["rstd"], op0=ALU.mult, op1=ALU.mult),
             reads=[k("mv"), k("rstd")], writes=[k("nmr")], dur=0.1)
        P.op("act", lambda e: e.activation(out=xnx, in_=R[:, n, :], func=AF.Identity, bias=L_["nmr"][:, 0:1], scale=L_["rstd"][:, 0:1]),
             reads=[src_key, k("rstd"), k("nmr")], writes=[xkey], dur=c_act(1024, 2))
        if affine:
            P.op("dve", lambda e: e.tensor_tensor(out=xnx, in0=xnx, in1=lng, op=ALU.mult), reads=[xkey, "lng:"], writes=[xkey], dur=c_dve(1024))
            P.op("pool", lambda e: e.tensor_tensor(out=dst, in0=xnx, in1=lnb, op=ALU.add), reads=[xkey, "lnb:"], writes=[dst_key], dur=c_pool(1024))

    def p2_tile(n):
        xb = n % 2
        tb = n // 4
        pb = 2 * (n % 2)
        P.dma(sp, lambda e: e.dma_start(out=xt[xb], in_=x_d[n * 128:(n + 1) * 128, :]), writes=["xt%d:" % xb], nbytes=512 * 1024)

        def mm_out(e):
            ins = None
            for half in range(2):
                for kc in range(8):
                    ins = e.matmul(ps[pb + half], lhsT=yT[:, kc, n * 128:(n + 1) * 128], rhs=wout[:, kc, half * 512:(half + 1) * 512],
                                   start=(kc == 0), stop=(kc == 7))
            return ins
        P.op("pe", mm_out, reads=["yT:c%d_%d" % (cc, tb) for cc in range(4)] + ["yT:r%d" % n] + ["wout:%d" % kc for kc in range(8)],
             writes=["ps%d" % pb, "ps%d" % (pb + 1)], dur=c_mm(16))
        for half in range(2):
            hs = slice(half * 512, (half + 1) * 512)
            P.op("dve", (lambda e, half=half, hs=hs: e.scalar_tensor_tensor(out=R[:, n, hs], in0=xt[xb][:, hs], scalar=ALPHA, in1=ps[pb + half],
                                                                         op0=ALU.mult, op1=ALU.add)),
                 reads=["xt%d:" % xb], writes=["ps%d" % (pb + half), "R:%d" % n], dur=c_dve(512))
        ln_tile(n, l1[xb], R[:, n, :], "R:%d" % n, None, None, affine=False)

    def p2_h2T(tb):
        for dc in range(8):
            bk = 4 + dc % 4

            def tr2(e, dc=dc, bk=bk):
                ins = None
                for t in range(4):
                    ins = e.transpose(ps[bk][:, t * 128:(t + 1) * 128], R[:, tb * 4 + t, dc * 128:(dc + 1) * 128], identf)
                return ins
            P.op("pe", tr2, reads=["R:%d" % (tb * 4 + t) for t in range(4)] + ["identf:"], writes=["ps%d" % bk], dur=0.95)
            P.op("act", (lambda e, dc=dc, bk=bk: e.activation(
                out=h2T[:, dc, tb * 512:(tb + 1) * 512], in_=ps[bk], func=AF.Identity,
                bias=mod2[:, 8 + dc:9 + dc], scale=mod2[:, dc:dc + 1])),
                reads=["mod2:s", "mod2:b"], writes=["ps%d" % bk, "h2T:%d_%d" % (dc, tb)], dur=c_act(512, 2))

    for n in range(NT):
        p2_tile(n)
        if n % 4 == 3:
            p2_h2T(n // 4)

    H2T = ["h2T:%d_%d" % (dc, tb) for dc in range(8) for tb in range(NB)]
    NH = S // 1024 if S >= 1024 else 1
    HW = S // NH
    NBH = HW // 512
    q = 0
    sgi = 0
    for gi, (f0, f1) in enumerate(FF_GROUPS):
        ng = f1 - f0
        last = (gi == len(FF_GROUPS) - 1)
        if gi == 1:
            P.dma(sp, lambda e: e.dma_start(out=lng, in_=ln2g_d.partition_broadcast(128)[:, 0, :]), writes=["lng:"], nbytes=512 * 1024)
            P.dma(sp, lambda e: e.dma_start(out=lnb, in_=ln2b_d.partition_broadcast(128)[:, 0, :]), writes=["lnb:"], nbytes=512 * 1024)
        for fi in range(ng):
            wsb = (f0 + fi) % 2
            P.dma(sp, (lambda e, fi=fi, f0=f0, wsb=wsb: e.dma_start(out=wst3[wsb], in_=wd_d[(f0 + fi) * 128:(f0 + fi + 1) * 128, :])),
                  writes=["wst3%d:" % wsb], nbytes=512 * 1024)
            P.op("dve", (lambda e, fi=fi, wsb=wsb: e.tensor_tensor(out=wd_s[:, fi, :], in0=wst3[wsb], in1=G_f, op=ALU.mult)),
                 reads=["wst3%d:" % wsb, "Gf:0", "Gf:1"], writes=["wd_s:%d" % fi], dur=c_dve(1024))
        def load_wgu(fc, wb):
            P.dma("pool", lambda e: e.dma_start(out=wgu[wb][:, 0, :, :], in_=wg_v[:, :, fc * 128:(fc + 1) * 128]),
                  writes=["wgu%d:g" % wb], nbytes=512 * 1024)
            P.dma("pool", lambda e: e.dma_start(out=wgu[wb][:, 1, :, :], in_=wu_v[:, :, fc * 128:(fc + 1) * 128]),
                  writes=["wgu%d:u" % wb], nbytes=512 * 1024)

        def gate_up(fi, hf, wb, sb_, ng=ng):
            pb = 4 * (hf % 2)

            def mm_gu(e):
                ins = None
                for gu in range(2):
                    for kc in range(8):
                        for b in range(NBH):
                            t0 = hf * HW + b * 512
                            ins = e.matmul(ps[pb + gu * 2 + b], lhsT=wgu[wb][:, gu, kc, :], rhs=h2T[:, kc, t0:t0 + 512],
                                           start=(kc == 0), stop=(kc == 7))
                return ins
            banks = ["ps%d" % (pb + gu * 2 + b) for gu in range(2) for b in range(NBH)]
            h2r = ["h2T:%d_%d" % (dc, (hf * HW) // 512 + b) for dc in range(8) for b in range(NBH)]
            P.op("pe", mm_gu, reads=h2r + ["wgu%d:g" % wb, "wgu%d:u" % wb], writes=banks, dur=c_mm(16 * NBH))
            for b in range(NBH):
                t0 = hf * HW + b * 512
                P.op("act", (lambda e, b=b: e.activation(out=sgate[sb_][:, b * 512:(b + 1) * 512], in_=ps[pb + b], func=AF.Silu)),
                     writes=["ps%d" % (pb + b), "sgate%d:%d" % (sb_, b)], dur=c_act(512))
                P.op("dve", (lambda e, b=b, t0=t0: e.tensor_tensor(
                    out=aT[:, fi, t0:t0 + 512], in0=ps[pb + 2 + b], in1=sgate[sb_][:, b * 512:(b + 1) * 512], op=ALU.mult)),
                    reads=["sgate%d:%d" % (sb_, b)], writes=["ps%d" % (pb + 2 + b), "aT:%d_%d" % (fi, t0 // 512)], dur=c_dve(512))

        def down(n, nbanks, gi=gi, ng=ng, last=last):
            pb = 2 * (n % nbanks)
            xb = n % 2

            def mm_dn(e):
                ins = None
                for half in range(2):
                    for fi in range(ng):
                        ins = e.matmul(ps[pb + half], lhsT=aT[:, fi, n * 128:(n + 1) * 128], rhs=wd_s[:, fi, half * 512:(half + 1) * 512],
                                       start=(fi == 0), stop=(fi == ng - 1))
                return ins
            P.op("pe", mm_dn, reads=["aT:%d_%d" % (fi, n // 4) for fi in range(ng)] + ["wd_s:%d" % fi for fi in range(ng)],
                 writes=["ps%d" % pb, "ps%d" % (pb + 1)], dur=c_mm(2 * ng))
            for half in range(2):
                hs = slice(half * 512, (half + 1) * 512)
                if gi == 0:
                    P.op("dve", (lambda e, half=half, hs=hs: e.scalar_tensor_tensor(
                        out=R[:, n, hs], in0=R[:, n, hs], scalar=ALPHA, in1=ps[pb + half], op0=ALU.mult, op1=ALU.add)),
                        reads=["R:%d" % n], writes=["ps%d" % (pb + half), "R:%d" % n], dur=c_dve(512))
                else:
                    P.op("dve", (lambda e, half=half, hs=hs: e.tensor_tensor(
                        out=R[:, n, hs], in0=ps[pb + half], in1=R[:, n, hs], op=ALU.add)),
                        reads=["R:%d" % n], writes=["ps%d" % (pb + half), "R:%d" % n], dur=c_dve(512))
            if last:
                ln_tile(n, l2[xb], xn3[xb], "xn3%d:" % xb, xn3[xb], "xn3%d:" % xb)
                P.dma(sp, lambda e: e.dma_start(out=out_d[n * 128:(n + 1) * 128, :], in_=xn3[xb]),
                      reads=["xn3%d:" % xb], writes=["out%d" % n], nbytes=512 * 1024)

        if not last or NH == 1:
            for fi in range(ng):
                wb = q % 3
                q += 1
                load_wgu(f0 + fi, wb)
                for hf in range(NH):
                    gate_up(fi, hf, wb, sgi % 2)
                    sgi += 1
                if gi == 0 and fi < 4:
                    for n in range(fi * NT // 4, (fi + 1) * NT // 4):
                        dep = ["aT:%d_%d" % (fi, b_) for b_ in range(NB)]
                        P.op("dve", (lambda e, n=n: e.tensor_tensor(out=R[:, n, :], in0=R[:, n, :], in1=lng, op=ALU.mult)),
                             reads=["R:%d" % n, "lng:"] + dep, writes=["R:%d" % n], dur=c_dve(1024))
                        P.op("pool", (lambda e, n=n: e.tensor_tensor(out=R[:, n, :], in0=R[:, n, :], in1=lnb, op=ALU.add)),
                             reads=["R:%d" % n, "lnb:"], writes=["R:%d" % n], dur=c_pool(1024))
            for n in range(NT):
                down(n, 4)
        else:
            TPH = NT // NH
            for hf in range(NH):
                for fi in range(ng):
                    wb = q % 3
                    q += 1
                    load_wgu(f0 + fi, wb)
                    gate_up(fi, hf, wb, sgi % 2)
                    sgi += 1
                for n in range(hf * TPH, (hf + 1) * TPH):
                    down(n, 2 if hf + 1 < NH else 4)

    P.emit()
    nc._P = P
    nc._prog_stats = (P.stats, len(P.ops), A.peak, p1_end, p2_end, p3_end, getattr(P, "t_est", None))
    return nc


def _constants(S):
    NT = S // 128
    ident_f = np.eye(128, dtype=np.float32)
    ident_b = np.eye(128, dtype=np.float32).astype(ml_dtypes.bfloat16)
    j = np.arange(128)[:, None]
    i = np.arange(128)[None, :]
    maskT = (i >= j).astype(np.float32)
    h = np.arange(4, dtype=np.float64)
    log_g = np.log1p(-(2.0 ** (-5.0 - h)))
    gL = np.exp(log_g * L)
    gLt = np.broadcast_to(np.repeat(gL, 128)[None, :], (128, 512)).astype(np.float32).copy()
    inv_freq = (10000.0 ** (-np.arange(0, HD, 2, dtype=np.float32) / np.float32(HD))).astype(np.float32)
    pos = np.arange(S, dtype=np.float32)
    ang = (pos[:, None] * inv_freq[None, :]).astype(np.float32)
    cos = np.cos(ang.astype(np.float64))
    sin = np.sin(ang.astype(np.float64))
    ii = (np.arange(S) % L).astype(np.float64)
    dq = np.exp(log_g[None, :] * ii[:, None])
    dk = np.exp(-log_g[None, :] * ii[:, None]) * (HD ** -0.5)
    rope = np.zeros((S, 1536), np.float64)
    cq = dq[:, :, None] * cos[:, None, :]
    sq = dq[:, :, None] * sin[:, None, :]
    ck = dk[:, :, None] * cos[:, None, :]
    sk = dk[:, :, None] * sin[:, None, :]
    rope[:, 0:256] = cq.reshape(S, 256)
    rope[:, 256:768] = np.stack([-sq, sq], axis=2).reshape(S, 512)
    rope[:, 768:1024] = ck.reshape(S, 256)
    rope[:, 1024:1536] = np.stack([-sk, sk], axis=2).reshape(S, 512)
    rope = rope.astype(np.float32).reshape(NT, 128, 1536)
    return dict(ident_f=ident_f, ident_b=ident_b, maskT=maskT, gLt=gLt, rope=rope)


_NC_CACHE = {}


def make_in_maps(S, x, c, ada_w, ada_b, w_in, conv_w, w_out, ln1_g, ln1_b, w_gate, w_up, w_down, ln2_g, ln2_b):
    B = x.shape[0]
    cst = _constants(S)
    f = lambda a: np.ascontiguousarray(np.asarray(a, dtype=np.float32))
    shared = dict(
        ada_w=f(ada_w[0]), ada_bT=f(np.asarray(ada_b[0]).reshape(48, 128).T),
        w_in=f(w_in[0]),
        convw_l=f(np.asarray(conv_w[0]).reshape(3, 4, 128).transpose(2, 1, 0).reshape(128, 12)),
        w_out=f(w_out[0]), ln1_g=f(ln1_g[0]).reshape(1, -1), ln1_b=f(ln1_b[0]).reshape(1, -1),
        ln1_gT=f(np.asarray(ln1_g[0]).reshape(8, 128).T), ln1_bT=f(np.asarray(ln1_b[0]).reshape(8, 128).T),
        w_gate=f(w_gate[0]), w_up=f(w_up[0]), w_down=f(w_down[0]),
        ln2_g=f(ln2_g[0]).reshape(1, -1), ln2_b=f(ln2_b[0]).reshape(1, -1), **cst)
    in_maps = []
    for b in range(B):
        m = dict(shared)
        m["x"] = f(x[b])
        m["c_l"] = f(np.asarray(c[b]).reshape(8, 128).T)
        in_maps.append(m)
    return in_maps


def kernel(x, c, ada_w, ada_b, w_in, conv_w, w_out, ln1_g, ln1_b, w_gate, w_up, w_down, ln2_g, ln2_b):
    x = np.asarray(x)
    B, S, _ = x.shape
    in_maps = make_in_maps(S, x, c, ada_w, ada_b, w_in, conv_w, w_out, ln1_g, ln1_b, w_gate, w_up, w_down, ln2_g, ln2_b)
    if S not in _NC_CACHE:
        _NC_CACHE[S] = build(S)
    nc = _NC_CACHE[S]
    res = run_bass_kernel_spmd(nc, in_maps, core_ids=list(range(B)))
    return np.stack([np.asarray(r["out"], dtype=np.float32) for r in res.results], axis=0)
```
